# Optimizing a Trainium2 kernel written in Bass

```python
import math
import jax, jax.numpy as jnp
from jax import lax
import numpy as np

D_MODEL = 1024
BATCH = 4
SEQ = 8192
DEPTH = 4

N_MEM = 256
HEAD_DIM = 64
ROT_DIM = HEAD_DIM // 4
ROPE_THETA = 500000.0
EPS = 1e-6
NEG_INF = -1e30
N_MIXERS = 3

A_HEADS = D_MODEL // HEAD_DIM
A_WIDTH = A_HEADS * HEAD_DIM
A_PATTERNS = ((128, 1), (512, 4), (2048, 16))
A_BLOCK = 64

B_HEADS = D_MODEL // (2 * HEAD_DIM)
B_WIDTH = B_HEADS * 2 * HEAD_DIM
B_QBLOCK = 128

C_HEADS = D_MODEL // HEAD_DIM
C_KV_HEADS = max(1, C_HEADS // 8)
C_GROUP = C_HEADS // C_KV_HEADS
C_Q_WIDTH = C_HEADS * HEAD_DIM
C_KV_WIDTH = C_KV_HEADS * HEAD_DIM
C_RADIUS = 128
C_BLOCK = 128

X_HEADS = 4
X_HEAD_DIM = D_MODEL // X_HEADS

_FF_RAW = -(-8 * D_MODEL // 3)
D_FF = -(-_FF_RAW // 256) * 256

N_A = len(range(0, DEPTH, N_MIXERS))
N_B = len(range(1, DEPTH, N_MIXERS))
N_C = len(range(2, DEPTH, N_MIXERS))

kernel_name = "hybrid_dilated_diff_swa_encoder"


def rms_norm(x, g):
    xf = x.astype(jnp.float32)
    y = xf * lax.rsqrt(jnp.mean(xf * xf, axis=-1, keepdims=True) + EPS)
    return (y * g.astype(jnp.float32)).astype(x.dtype)


def split_heads(t, n):
    b, s, _ = t.shape
    return t.reshape(b, s, n, -1).transpose(0, 2, 1, 3)


def merge_heads(t):
    b, n, s, d = t.shape
    return t.transpose(0, 2, 1, 3).reshape(b, s, n * d)


def rope_tables(positions):
    inv_freq = ROPE_THETA ** (-jnp.arange(0, ROT_DIM, 2, dtype=jnp.float32) / ROT_DIM)
    ang = positions.astype(jnp.float32)[..., None] * inv_freq
    return jnp.cos(ang), jnp.sin(ang)


def apply_partial_rope(t, cos, sin):
    shape = (cos.shape[0],) + (1,) * (t.ndim - 3) + cos.shape[1:]
    c = cos.reshape(shape).astype(t.dtype)
    s = sin.reshape(shape).astype(t.dtype)
    half = ROT_DIM // 2
    t1, t2, rest = t[..., :half], t[..., half:ROT_DIM], t[..., ROT_DIM:]
    return jnp.concatenate([t1 * c - t2 * s, t2 * c + t1 * s, rest], axis=-1)


def banded_attention(q, k, v, radius, block, sink=None):
    L = q.shape[-2]
    nb = -(-L // block)
    pad = nb * block - L

    def pad_seq(t, lo, hi):
        return jnp.pad(t, [(0, 0)] * (t.ndim - 2) + [(lo, hi), (0, 0)])

    qb = pad_seq(q, 0, pad).reshape(q.shape[:-2] + (nb, block, q.shape[-1]))

    def band(t):
        tp = pad_seq(t, block, block + pad).reshape(t.shape[:-2] + (nb + 2, block, t.shape[-1]))
        return jnp.concatenate([tp[..., :-2, :, :], tp[..., 1:-1, :, :], tp[..., 2:, :, :]], axis=-2)

    kb, vb = band(k), band(v)
    s = jnp.einsum('...gnqd,...nkd->...gnqk', qb, kb,
                   preferred_element_type=jnp.float32) * (q.shape[-1] ** -0.5)
    qpos = jnp.arange(nb * block).reshape(nb, block, 1)
    kpos = (jnp.arange(nb)[:, None, None] - 1) * block + jnp.arange(3 * block)[None, None, :]
    valid = (jnp.abs(kpos - qpos) <= radius) & (kpos >= 0) & (kpos < L)
    s = jnp.where(valid, s, NEG_INF)
    m = jnp.max(s, axis=-1, keepdims=True)
    if sink is not None:
        sink = sink.astype(jnp.float32)
        m = jnp.maximum(m, sink)
    p = jnp.exp(s - m)
    denom = jnp.sum(p, axis=-1, keepdims=True)
    if sink is not None:
        denom = denom + jnp.exp(sink - m)
    o = jnp.einsum('...gnqk,...nkd->...gnqd', p.astype(v.dtype), vb,
                   preferred_element_type=jnp.float32) / denom
    lse = (m + jnp.log(denom))[..., 0]
    o = o.reshape(o.shape[:-3] + (nb * block, o.shape[-1]))[..., :L, :]
    lse = lse.reshape(lse.shape[:-2] + (nb * block,))[..., :L]
    return o, lse


def dilated_attention_mixer(u, w_in, w_out, cos, sin):
    b, s_len, _ = u.shape
    q, k, v = jnp.split(u @ w_in, 3, axis=-1)
    q = apply_partial_rope(split_heads(q, A_HEADS), cos, sin)
    k = apply_partial_rope(split_heads(k, A_HEADS), cos, sin)
    v = split_heads(v, A_HEADS)
    outs, lses = [], []
    for window, dil in A_PATTERNS:
        radius = window // (2 * dil)

        def by_residue(t):
            return t.reshape(b, A_HEADS, s_len // dil, dil, HEAD_DIM).swapaxes(2, 3)

        o, lse = banded_attention(by_residue(q)[..., None, :, :], by_residue(k), by_residue(v),
                                  radius, A_BLOCK)
        outs.append(o[..., 0, :, :].swapaxes(2, 3).reshape(b, A_HEADS, s_len, HEAD_DIM))
        lses.append(lse[..., 0, :].swapaxes(2, 3).reshape(b, A_HEADS, s_len))
    w = jax.nn.softmax(jnp.stack(lses), axis=0)
    o = jnp.einsum('pbhs,pbhsd->bhsd', w, jnp.stack(outs))
    return merge_heads(o).astype(u.dtype) @ w_out


def differential_attention_mixer(u, w_in, w_out, lam_q1, lam_k1, lam_q2, lam_k2, sub_g,
                                 cos, sin, layer_idx):
    b, s_len, _ = u.shape
    q, k, v = jnp.split(u @ w_in, 3, axis=-1)

    def qk_heads(t):
        return t.reshape(b, s_len, B_HEADS, 2, HEAD_DIM).transpose(0, 2, 3, 1, 4)

    q = apply_partial_rope(qk_heads(q), cos, sin)
    k = apply_partial_rope(qk_heads(k), cos, sin)
    v = split_heads(v, B_HEADS)
    lam_init = 0.8 - 0.6 * math.exp(-0.3 * layer_idx)
    f32 = jnp.float32
    lam = (jnp.exp(jnp.sum(lam_q1.astype(f32) * lam_k1.astype(f32)))
           - jnp.exp(jnp.sum(lam_q2.astype(f32) * lam_k2.astype(f32))) + lam_init)
    nqb = s_len // B_QBLOCK
    qb = q.reshape(b, B_HEADS, 2, nqb, B_QBLOCK, HEAD_DIM).transpose(3, 0, 1, 2, 4, 5)
    scale = HEAD_DIM ** -0.5

    def one_block(qblk):
        sc = jnp.einsum('bhcqd,bhckd->bhcqk', qblk, k, preferred_element_type=f32) * scale
        p = jax.nn.softmax(sc, axis=-1)
        a = p[:, :, 0] - lam * p[:, :, 1]
        return jnp.einsum('bhqk,bhkd->bhqd', a.astype(v.dtype), v, preferred_element_type=f32)

    o = lax.map(one_block, qb)
    o = o.transpose(1, 2, 0, 3, 4).reshape(b, B_HEADS, s_len, 2 * HEAD_DIM)
    o = rms_norm(o, sub_g) * (1.0 - lam_init)
    return merge_heads(o).astype(u.dtype) @ w_out


def windowed_gqa_sink_mixer(u, w_in, w_out, sink, cos, sin):
    b, s_len, _ = u.shape
    qkv = u @ w_in
    q = qkv[..., :C_Q_WIDTH]
    k = qkv[..., C_Q_WIDTH:C_Q_WIDTH + C_KV_WIDTH]
    v = qkv[..., C_Q_WIDTH + C_KV_WIDTH:]
    q = apply_partial_rope(split_heads(q, C_HEADS), cos, sin)
    k = apply_partial_rope(split_heads(k, C_KV_HEADS), cos, sin)
    v = split_heads(v, C_KV_HEADS)
    q = q.reshape(b, C_KV_HEADS, C_GROUP, s_len, HEAD_DIM)
    o, _ = banded_attention(q, k, v, C_RADIUS, C_BLOCK,
                            sink=sink.reshape(1, C_KV_HEADS, C_GROUP, 1, 1, 1))
    o = o.reshape(b, C_HEADS, s_len, HEAD_DIM)
    return merge_heads(o).astype(u.dtype) @ w_out


def memory_cross_attention(u, mem_n, wq, wkv, wo):
    q = split_heads(u @ wq, X_HEADS)
    k, v = jnp.split(mem_n @ wkv, 2, axis=-1)
    k = split_heads(k, X_HEADS)
    v = split_heads(v, X_HEADS)
    s = jnp.einsum('bhqd,bhkd->bhqk', q, k,
                   preferred_element_type=jnp.float32) * (X_HEAD_DIM ** -0.5)
    p = jax.nn.softmax(s, axis=-1)
    o = jnp.einsum('bhqk,bhkd->bhqd', p.astype(v.dtype), v)
    return merge_heads(o) @ wo


def swiglu_ffn(u, w_gate_up, w_down):
    g, up = jnp.split(u @ w_gate_up, 2, axis=-1)
    return (jax.nn.silu(g) * up) @ w_down


def setup_inputs(seed: int = 0) -> dict:
    key = jax.random.key(seed)
    ks = iter(jax.random.split(key, 32))

    def nrm(shape, scale):
        return jax.random.normal(next(ks), shape, jnp.float32) * scale

    def dense(shape):
        return nrm(shape, shape[-2] ** -0.5)

    def gain(shape):
        return 1.0 + nrm(shape, 0.05)

    x = nrm((BATCH, SEQ, D_MODEL), 1.0)
    mem = nrm((BATCH, N_MEM, D_MODEL), 1.0)
    offset = jax.random.randint(next(ks), (BATCH, 1), 0, 4096, dtype=jnp.int32)
    positions = jnp.arange(SEQ, dtype=jnp.int32)[None, :] + offset
    return {
        "x": x, "mem": mem, "positions": positions,
        "mix_pre_g": gain((DEPTH, D_MODEL)), "mix_post_g": gain((DEPTH, D_MODEL)),
        "mem_pre_g": gain((DEPTH, D_MODEL)), "mem_kv_g": gain((DEPTH, D_MODEL)),
        "mem_post_g": gain((DEPTH, D_MODEL)),
        "ffn_pre_g": gain((DEPTH, D_MODEL)), "ffn_post_g": gain((DEPTH, D_MODEL)),
        "a_w_in": dense((N_A, D_MODEL, 3 * A_WIDTH)), "a_w_out": dense((N_A, A_WIDTH, D_MODEL)),
        "b_w_in": dense((N_B, D_MODEL, 3 * B_WIDTH)), "b_w_out": dense((N_B, B_WIDTH, D_MODEL)),
        "b_lam_q1": nrm((N_B, HEAD_DIM), 0.1), "b_lam_k1": nrm((N_B, HEAD_DIM), 0.1),
        "b_lam_q2": nrm((N_B, HEAD_DIM), 0.1), "b_lam_k2": nrm((N_B, HEAD_DIM), 0.1),
        "b_sub_g": gain((N_B, 2 * HEAD_DIM)),
        "c_w_in": dense((N_C, D_MODEL, C_Q_WIDTH + 2 * C_KV_WIDTH)),
        "c_w_out": dense((N_C, C_Q_WIDTH, D_MODEL)),
        "c_sink": nrm((N_C, C_HEADS), 0.5),
        "x_wq": dense((DEPTH, D_MODEL, D_MODEL)), "x_wkv": dense((DEPTH, D_MODEL, 2 * D_MODEL)),
        "x_wo": dense((DEPTH, D_MODEL, D_MODEL)),
        "w_gate_up": dense((DEPTH, D_MODEL, 2 * D_FF)), "w_down": dense((DEPTH, D_FF, D_MODEL)),
    }


def reference(x, mem, positions, mix_pre_g, mix_post_g, mem_pre_g, mem_kv_g, mem_post_g,
              ffn_pre_g, ffn_post_g, a_w_in, a_w_out, b_w_in, b_w_out, b_lam_q1, b_lam_k1,
              b_lam_q2, b_lam_k2, b_sub_g, c_w_in, c_w_out, c_sink, x_wq, x_wkv, x_wo,
              w_gate_up, w_down):
    cos, sin = rope_tables(positions)
    h = x
    for i in range(DEPTH):
        kind, j = i % N_MIXERS, i // N_MIXERS
        u = rms_norm(h, mix_pre_g[i])
        if kind == 0:
            y = dilated_attention_mixer(u, a_w_in[j], a_w_out[j], cos, sin)
        elif kind == 1:
            y = differential_attention_mixer(u, b_w_in[j], b_w_out[j], b_lam_q1[j], b_lam_k1[j],
                                             b_lam_q2[j], b_lam_k2[j], b_sub_g[j], cos, sin, i)
        else:
            y = windowed_gqa_sink_mixer(u, c_w_in[j], c_w_out[j], c_sink[j], cos, sin)
        h = h + rms_norm(y, mix_post_g[i])
        u = rms_norm(h, mem_pre_g[i])
        y = memory_cross_attention(u, rms_norm(mem, mem_kv_g[i]), x_wq[i], x_wkv[i], x_wo[i])
        h = h + rms_norm(y, mem_post_g[i])
        u = rms_norm(h, ffn_pre_g[i])
        y = swiglu_ffn(u, w_gate_up[i], w_down[i])
        h = h + rms_norm(y, ffn_post_g[i])
    return h
```

```python
import math
import numpy as np
from contextlib import ExitStack
import concourse.bass as bass
import concourse.mybir as mybir
from concourse.bass_utils import run_bass_kernel_spmd

F32 = mybir.dt.float32
BF16 = mybir.dt.bfloat16
I32 = mybir.dt.int32
ALU = mybir.AluOpType
AF = mybir.ActivationFunctionType
AX = mybir.AxisListType

D = 1024
S = 8192
HALF = 4096
NMEM = 256
DFF = 2816
EPS = 1e-6
ROPE_THETA = 500000.0
A_DILS = (1, 4, 16)


class Buf:
    __slots__ = ("name", "last_w", "reads", "dsem", "dcount")

    def __init__(self, name):
        self.name = name
        self.last_w = None
        self.reads = []
        self.dsem = None
        self.dcount = 0


class Op:
    __slots__ = ("eng", "fn", "deps", "need", "tok", "dma", "buf", "inc")

    def __init__(self, eng, fn, deps, dma=False, buf=None, inc=16):
        self.inc = inc
        self.eng = eng
        self.fn = fn
        self.deps = deps
        self.need = False
        self.tok = None
        self.dma = dma
        self.buf = buf


class Prog:
    ENGS = ("pe", "act", "dve", "pool", "sp")
    SEM_ROT = 30000

    def __init__(self, nc, es):
        self.nc = nc
        self.es = es
        self.ops = {e: [] for e in self.ENGS}
        self.nsem = 0
        self.last = {e: None for e in self.ENGS}
        self.dma_since_bar = []
        self.named = {}

    def new_sem(self, name):
        self.nsem += 1
        return self.es.enter_context(self.nc.semaphore(f"{name}_{self.nsem}"))

    def buf(self, name=None):
        if name is None:
            return Buf("anon")
        if name not in self.named:
            self.named[name] = Buf(name)
        return self.named[name]

    def bufs(self, n, name=None):
        if name is None:
            return [Buf("anon") for _ in range(n)]
        return [self.buf(f"{name}{i}") for i in range(n)]

    def fresh(self, n, name="dram"):
        return [Buf(name) for _ in range(n)]

    @staticmethod
    def _deps(reads, writes):
        deps = []
        for b in reads:
            if b.last_w is not None:
                deps.append(b.last_w)
        for b in writes:
            if b.last_w is not None:
                deps.append(b.last_w)
            deps.extend(b.reads)
        return deps

    @staticmethod
    def _commit(op, reads, writes):
        for b in reads:
            b.reads.append(op)
        for b in writes:
            b.last_w = op
            b.reads = []

    def op(self, eng, fn, reads=(), writes=()):
        o = Op(eng, fn, self._deps(reads, writes))
        self._commit(o, reads, writes)
        self.ops[eng].append(o)
        self.last[eng] = o
        return o

    def dma(self, eng, fn, sb, reads=(), writes=(), inc=16):
        ww = [sb] + list(writes)
        o = Op(eng, fn, self._deps(list(reads), ww), dma=True, buf=sb, inc=inc)
        self._commit(o, list(reads), ww)
        self.ops[eng].append(o)
        self.dma_since_bar.append(o)
        return o

    def barrier(self):
        deps = [o for o in self.last.values() if o is not None] + list(self.dma_since_bar)
        self.dma_since_bar = []
        for e in self.ENGS:
            o = Op(e, None, list(deps))
            self.ops[e].append(o)

    def finish(self, final_bufs):
        nc = self.nc
        for e in self.ENGS:
            for o in self.ops[e]:
                for d in o.deps:
                    if d.dma:
                        continue
                    if d.eng == "pe" and o.eng == "pe" and not o.dma and o.fn is not None:
                        continue
                    d.need = True
        for e in self.ENGS:
            cnt = 0
            sem = None
            for o in self.ops[e]:
                if o.dma:
                    b = o.buf
                    if b.dsem is None:
                        b.dsem = self.new_sem("d")
                    b.dcount += o.inc
                    o.tok = (b.dsem, b.dcount)
                elif o.need:
                    if sem is None or cnt >= self.SEM_ROT:
                        sem = self.new_sem(e)
                        cnt = 0
                    cnt += 1
                    o.tok = (sem, cnt)
        engmap = {"pe": "tensor", "act": "scalar", "dve": "vector", "pool": "gpsimd", "sp": "sync"}
        final_toks = [b.last_w.tok for b in final_bufs if b.last_w is not None and b.last_w.tok is not None]

        def make_body(e):
            def body(eng):
                known = {}
                for o in self.ops[e]:
                    need = {}
                    for d in o.deps:
                        if d.tok is None:
                            continue
                        if (not d.dma) and d.eng == "pe" and e == "pe" and not o.dma and o.fn is not None:
                            continue
                        s, v = d.tok
                        if known.get(id(s), 0) >= v:
                            continue
                        cur = need.get(id(s))
                        if cur is None or cur[1] < v:
                            need[id(s)] = (s, v)
                    for s, v in need.values():
                        eng.wait_ge(s, v)
                        known[id(s)] = v
                    if o.fn is None:
                        continue
                    ins = o.fn(eng)
                    if isinstance(ins, list):
                        for i_ in ins:
                            i_.then_inc(o.tok[0], o.inc // len(ins))
                    elif o.tok is not None:
                        ins.then_inc(o.tok[0], o.inc if o.dma else 1)
                if e == "sp":
                    for s, v in final_toks:
                        if known.get(id(s), 0) >= v:
                            continue
                        eng.wait_ge(s, v)
                        known[id(s)] = v
            return body

        with nc.Block() as block:
            for e in self.ENGS:
                getattr(block, engmap[e])(make_body(e))
        return {e: len(self.ops[e]) for e in self.ENGS}


def ssl(start, n, step=1):
    return slice(start, start + (n - 1) * step + 1, step)


def layer_kind(i):
    return i % 3


def mixer_cols(kind):
    return 1280 if kind == 2 else 3072


class Builder:
    def __init__(self, layers, n_other_in):
        self.layers = layers
        nc = self.nc = bass.Bass("TRN2", target_bir_lowering=False)
        self.es = ExitStack()
        self.P = Prog(nc, self.es)
        self.din = {}
        self._io()

    def _in(self, name, shape, dt=F32):
        t = self.nc.dram_tensor(name, list(shape), dt, kind="ExternalInput").ap()
        self.din[name] = t
        return t

    def _io(self):
        nc = self.nc
        self.hin = self._in("hin", [S, D])
        self.pos = self._in("pos", [128, 64], I32)
        self.mem = self._in("mem", [NMEM, D])
        self.sel = self._in("sel", [128, 2])
        self.hout = nc.dram_tensor("hout", [HALF, D], F32, kind="ExternalOutput").ap()
        self.W = {}
        for l in self.layers:
            k = layer_kind(l)
            w = {}
            w["g"] = self._in(f"g{l}", [7, D])
            w["w_in"] = self._in(f"w_in{l}", [D, mixer_cols(k)])
            w["w_out"] = self._in(f"w_out{l}", [D, D])
            w["wq"] = self._in(f"wq{l}", [D, D])
            w["wkv"] = self._in(f"wkv{l}", [D, 2 * D])
            w["wo"] = self._in(f"wo{l}", [D, D])
            w["wgu"] = self._in(f"wgu{l}", [D, 2 * DFF])
            w["wd"] = self._in(f"wd{l}", [DFF, D])
            if k == 1:
                w["lam"] = self._in(f"lam{l}", [1, 256])
                w["subg"] = self._in(f"subg{l}", [128, 1])
            if k == 2:
                w["sink"] = self._in(f"sink{l}", [16, 1])
            self.W[l] = w
        dk = dict(kind="ExternalOutput") if _DBG_H1 else {}
        self.qT_d = nc.dram_tensor("qT_d", [8, 128, HALF], BF16, **dk).ap()
        self.kT_d = nc.dram_tensor("kT_d", [8, 128, S], BF16, **dk).ap()
        self.v_d = nc.dram_tensor("v_d", [S, D], BF16, **dk).ap()
        if _DBG_H1:
            self.ot_d = nc.dram_tensor("ot_d", [128, 8, HALF], BF16, kind="ExternalOutput").ap()
        self.h1_d = (nc.dram_tensor("h1_d", [HALF, D], F32, kind="ExternalOutput").ap() if _DBG_H1 else nc.dram_tensor("h1_d", [HALF, D], F32).ap())
        self.hx_d = [nc.dram_tensor(f"hx_d{i}", [HALF, D], F32).ap() for i in range(2)]
        self.hoth_d = nc.dram_tensor("hoth_d", [HALF, D], F32).ap()
        self.G_d = {}
        for l in self.layers[1:]:
            n = {0: 8, 1: 32, 2: 4}[layer_kind(l)] * 128
            self.G_d[l] = nc.dram_tensor(f"G_d{l}", [2 * n, D], F32).ap()

    def alloc(self):
        nc, es, P = self.nc, self.es, self.P
        self.ARENA_ELEMS = 100 * 1024
        self.arena = es.enter_context(nc.sbuf_tensor("arena", [128, self.ARENA_ELEMS], BF16))
        self.PS = es.enter_context(nc.psum_tensor("PS", [128, 4096], F32))
        self.b_ps = P.bufs(8, "ps")
        self.ident = es.enter_context(nc.sbuf_tensor("ident", [128, 128], BF16))
        self.ones_bf = es.enter_context(nc.sbuf_tensor("ones_bf", [128, 128], BF16))
        self.ones_f = es.enter_context(nc.sbuf_tensor("ones_f", [128, 128], F32))
        self.cs = es.enter_context(nc.sbuf_tensor("cs", [128, 64, 16], F32))
        self.b_const = P.buf("const")
        self.b_cs = P.buf("cs")
        self.small = es.enter_context(nc.sbuf_tensor("small", [128, 64], F32))
        self.jmat = es.enter_context(nc.sbuf_tensor("jmat", [128, 128], F32))
        self.sel_sb = es.enter_context(nc.sbuf_tensor("sel_sb", [128, 2], F32))
        self.b_sel = P.buf("sel")
        self.apos = 0

    def bank(self, i, n=1):
        return self.PS[:, i * 512:(i + n) * 512]

    def bank_bf(self, i):
        return self.PS[:, i * 512:(i + 1) * 512].bitcast(BF16)

    def areset(self):
        self.apos = 0

    def abf(self, n):
        n = (n + 15) // 16 * 16
        assert self.apos + n <= self.ARENA_ELEMS, (self.apos, n)
        ap = self.arena[:, self.apos:self.apos + n]
        self.apos += n
        return ap

    def af32(self, n):
        return self.abf(2 * n).bitcast(F32)

    def consts(self):
        P = self.P
        ident, ones_bf, ones_f = self.ident, self.ones_bf, self.ones_f
        bc = self.b_const
        P.op("pool", lambda e: e.memset(ones_bf[:], 1.0), writes=[bc])
        P.op("pool", lambda e: e.memset(ones_f[:], 1.0), writes=[bc])
        P.op("pool", lambda e: e.memset(ident[:], 1.0), writes=[bc])
        P.op("pool", lambda e: e.affine_select(out=ident[:], in_=ident[:], pattern=[[-1, 128]],
                                                compare_op=ALU.is_equal, fill=0.0, base=0,
                                                channel_multiplier=1), reads=[bc], writes=[bc])
        jm = self.jmat
        P.op("pool", lambda e: e.memset(jm[:], 1.0), writes=[bc])
        P.op("pool", lambda e: e.affine_select(out=jm[:], in_=jm[:], pattern=[[1, 128]],
                                                compare_op=ALU.is_equal, fill=0.0, base=-127,
                                                channel_multiplier=1), reads=[bc], writes=[bc])
        P.dma("sp", lambda e: e.dma_start(out=self.sel_sb[:], in_=self.sel), self.b_sel)
        cs = self.cs
        self.areset()
        posi = self.abf(128).bitcast(I32)
        posf = self.af32(64)
        tt = self.af32(64 * 16).rearrange("p (t c) -> p t c", c=16)
        ti = self.abf(2 * 64 * 16).bitcast(I32).rearrange("p (t c) -> p t c", c=16)
        tr = self.af32(64 * 16).rearrange("p (t c) -> p t c", c=16)
        b = P.buf("ropetmp")
        P.dma("sp", lambda e: e.dma_start(out=posi, in_=self.pos), b)
        P.op("dve", lambda e: e.tensor_copy(out=posf, in_=posi), reads=[b], writes=[b])
        for i in range(8):
            inv = float(np.float32(ROPE_THETA) ** np.float32(-(2.0 * i) / 16.0))
            P.op("dve", lambda e, i=i, inv=inv: e.tensor_scalar(
                out=tt[:, :, 8 + i], in0=posf, scalar1=inv, scalar2=1.0 / (2 * math.pi),
                op0=ALU.mult, op1=ALU.mult), reads=[b], writes=[b])
        P.op("dve", lambda e: e.tensor_scalar(out=tt[:, :, 0:8], in0=tt[:, :, 8:16], scalar1=0.25,
                                               scalar2=None, op0=ALU.add), reads=[b], writes=[b])
        P.op("dve", lambda e: e.tensor_copy(out=ti, in_=tt), reads=[b], writes=[b])
        P.op("dve", lambda e: e.tensor_copy(out=tr, in_=ti), reads=[b], writes=[b])
        P.op("dve", lambda e: e.tensor_sub(out=tt, in0=tt, in1=tr), reads=[b], writes=[b])
        P.op("act", lambda e: e.activation(out=cs[:], in_=tt, func=AF.Sin, scale=2 * math.pi * (1 - 1e-6)),
             reads=[b], writes=[self.b_cs])
        P.barrier()

    def load_w(self, dst, src, b, nchunk):
        P = self.P
        srcv = src.rearrange("(k p) n -> p k n", p=128)
        for k in range(nchunk):
            P.dma("pool", lambda e, k=k: e.dma_start(out=dst[:, k, :], in_=srcv[:, k, :]), b)

    def load_g(self, dst, gsrc, row, b):
        self.P.dma("sp", lambda e: e.dma_start(out=dst, in_=gsrc[row:row + 1, :].partition_broadcast(128)), b)

    def rms_u(self, src, b_src, g_bc, b_g, u_out, b_u, ss, rstd, b_s, junk, b_junk):
        P = self.P
        P.op("act", lambda e: e.activation(out=junk, in_=src, func=AF.Square, accum_out=ss),
             reads=[b_src], writes=[b_junk, b_s])
        P.op("act", lambda e: e.activation(out=rstd, in_=ss, func=AF.Sqrt, scale=1.0 / D, bias=EPS),
             reads=[b_s], writes=[b_s])
        P.op("dve", lambda e: e.reciprocal(out=rstd, in_=rstd), reads=[b_s], writes=[b_s])
        P.op("dve", lambda e: e.scalar_tensor_tensor(out=u_out, in0=src, scalar=rstd, in1=g_bc,
                                                      op0=ALU.mult, op1=ALU.mult),
             reads=[b_src, b_s, b_g], writes=[b_u])

    def transpose8(self, src, b_src, bank_i, dst, b_dst, evac="act"):
        P = self.P
        pb = self.bank_bf(bank_i).rearrange("p (k t) -> p k t", k=8)
        for k in range(8):
            P.op("pe", lambda e, k=k: e.transpose(out=pb[:, k, :], in_=src[:, k * 128:(k + 1) * 128],
                                                   identity=self.ident[:]),
                 reads=[b_src, self.b_const], writes=[self.b_ps[bank_i]])
        if evac == "act":
            P.op("act", lambda e: e.copy(out=dst, in_=pb), reads=[self.b_ps[bank_i]], writes=[b_dst])
        else:
            P.op("dve", lambda e: e.tensor_copy(out=dst, in_=pb), reads=[self.b_ps[bank_i]], writes=[b_dst])

    def post_norm_residual(self, ybank0, hres, b_h, g_bc, b_g, ss, rstd, b_s, tmp, b_tmp):
        P = self.P
        y = self.bank(ybank0, 2)
        by = [self.b_ps[ybank0], self.b_ps[ybank0 + 1]]
        P.op("act", lambda e: e.activation(out=tmp, in_=y, func=AF.Square, accum_out=ss),
             reads=by, writes=[b_tmp, b_s])
        P.op("act", lambda e: e.activation(out=rstd, in_=ss, func=AF.Sqrt, scale=1.0 / D, bias=EPS),
             reads=[b_s], writes=[b_s])
        P.op("dve", lambda e: e.reciprocal(out=rstd, in_=rstd), reads=[b_s], writes=[b_s])
        for hb in range(2):
            cs_ = slice(hb * 512, (hb + 1) * 512)
            P.op("dve", lambda e, hb=hb, cs_=cs_: e.scalar_tensor_tensor(
                out=tmp[:, cs_], in0=self.bank(ybank0 + hb), scalar=rstd, in1=g_bc[:, cs_],
                op0=ALU.mult, op1=ALU.mult), reads=by + [b_s, b_g], writes=[b_tmp])
        P.op("pool", lambda e: e.tensor_add(out=hres, in0=hres, in1=tmp), reads=[b_tmp], writes=[b_h])

    def stage_m1(self, l, hsrc, b_hsrc, gath=None, b_gath=None):
        P, nc = self.P, self.nc
        kind = layer_kind(l)
        w = self.W[l]
        ncols = mixer_cols(kind)
        n_other = {0: 8, 1: 32, 2: 1}[kind]
        if gath is not None and kind == 2:
            n_other = 4
        self.areset()
        w_in = self.abf(8 * ncols).rearrange("p (k n) -> p k n", k=8)
        b_w = P.buf("w_in")
        self.load_w(w_in, w["w_in"], b_w, 8)
        g_bc = self.af32(D)
        b_g = P.buf("g")
        self.load_g(g_bc, w["g"], 0, b_g)
        NS = 3
        hts = [self.af32(D) for _ in range(NS)]
        b_ht = P.bufs(NS, "ht")
        gt0 = self.af32(D)
        gt1 = self.af32(D)
        b_gt = P.bufs(2, "gt")
        us = [self.abf(D) for _ in range(2)]
        b_us = P.bufs(2, "u")
        uTs = [self.abf(D).rearrange("p (k t) -> p k t", k=8) for _ in range(2)]
        b_uT = P.bufs(2, "uT")
        junk = self.abf(D)
        b_junk = P.buf("junk")
        sst = self.small
        b_ss = P.bufs(NS, "ss")
        qk_sb = [self.abf(2 * D) for _ in range(2)]
        b_qk = P.bufs(2, "qk")
        v_sb = [self.abf(D) for _ in range(2)]
        b_v = P.bufs(2, "v")
        rt = [self.af32(32 * 8 * 4).rearrange("p (a h c) -> p a h c", a=4, h=32) for _ in range(1)]
        b_rt = P.buf("rt")
        qT_st = [self.abf(8 * 512).rearrange("p (c t) -> p c t", c=8) for _ in range(2)]
        b_qT = P.bufs(2, "qTst")
        kT_st = [self.abf(8 * 512).rearrange("p (c t) -> p c t", c=8) for _ in range(2)]
        b_kT = P.bufs(2, "kTst")
        self.b_qTd = P.fresh(8)
        self.b_kTd = P.fresh(16)
        self.b_vd = P.fresh(64)
        nkc = 8 if kind != 2 else 1

        b_hoth = None
        if gath is not None:
            b_hoth = P.fresh(n_other, "hoth")
            hrev = self.af32(D)
            b_hrev = P.buf("hrev")
            n_g = n_other * 128
            cr = min(n_g, 512)
            for tt in range(n_other):
                row0 = n_g - 128 * (tt + 1)
                ch, off = divmod(row0, cr)
                ra = (ch * 2) * cr + off
                rb = (ch * 2 + 1) * cr + off
                P.dma("sp", lambda e, ra=ra: e.dma_start(out=gt0, in_=gath[ra:ra + 128, :]), b_gt[0], reads=[b_gath])
                P.dma("sp", lambda e, rb=rb: e.dma_start(out=gt1, in_=gath[rb:rb + 128, :]), b_gt[1], reads=[b_gath])
                P.op("dve", lambda e: e.tensor_scalar(out=gt0, in0=gt0, scalar1=self.sel_sb[:, 0:1], scalar2=None,
                                                       op0=ALU.mult), reads=[self.b_sel], writes=[b_gt[0]])
                P.op("dve", lambda e: e.scalar_tensor_tensor(out=gt0, in0=gt1, scalar=self.sel_sb[:, 1:2],
                                                              in1=gt0, op0=ALU.mult, op1=ALU.add),
                     reads=[self.b_sel, b_gt[1]], writes=[b_gt[0]])
                for cb in range(2):
                    P.op("pe", lambda e, cb=cb: e.matmul(self.bank(cb), lhsT=self.jmat[:],
                                                         rhs=gt0[:, cb * 512:(cb + 1) * 512], start=True, stop=True),
                         reads=[b_gt[0], self.b_const], writes=[self.b_ps[cb]])
                P.op("act", lambda e: e.copy(out=hrev, in_=self.bank(0, 2)), reads=self.b_ps[0:2], writes=[b_hrev])
                P.dma("sp", lambda e, tt=tt: e.dma_start(out=self.hoth_d[tt * 128:(tt + 1) * 128, :], in_=hrev),
                      b_hrev, writes=[b_hoth[tt]])
        tiles = list(range(32)) + [32 + i for i in range(n_other)]
        if gath is not None and "norev" in _DBG_SKIP:
            tiles = list(range(32))
        if _DBG_M1T is not None:
            tiles = tiles[:_DBG_M1T]
        for it, t in enumerate(tiles):
            own = t < 32
            s3 = it % NS
            s2 = it % 2
            ht = hts[s3]
            rev = False
            if own or gath is None:
                P.dma("sp", lambda e, t=t, ht=ht: e.dma_start(out=ht, in_=hsrc[t * 128:(t + 1) * 128, :]),
                      b_ht[s3], reads=[b_hsrc[min(t, len(b_hsrc) - 1)]])
            else:
                tt = t - 32
                P.dma("sp", lambda e, tt=tt, ht=ht: e.dma_start(out=ht, in_=self.hoth_d[tt * 128:(tt + 1) * 128, :]),
                      b_ht[s3], reads=[b_hoth[tt]])
            if _DBG_M1STOP <= 0:
                continue
            ss = sst[:, 2 * s3:2 * s3 + 1]
            rstd = sst[:, 2 * s3 + 1:2 * s3 + 2]
            self.rms_u(ht, b_ht[s3], g_bc, b_g, us[s2], b_us[s2], ss, rstd, b_ss[s3], junk, b_junk)
            if _DBG_M1STOP <= 1:
                continue
            if not rev:
                self.transpose8(us[s2], b_us[s2], 6 + s2, uTs[s2], b_uT[s2], evac="act")
            else:
                pj = self.bank(6, 2).rearrange("p (k t) -> p k t", k=8)
                for k in range(8):
                    P.op("pe", lambda e, k=k, pj=pj, uu=us[s2]: e.matmul(
                        pj[:, k, :], lhsT=uu[:, k * 128:(k + 1) * 128], rhs=self.jmat[:], start=True, stop=True),
                        reads=[b_us[s2], self.b_const], writes=[self.b_ps[6], self.b_ps[7]])
                P.op("act", lambda e, pj=pj, dst=uTs[s2]: e.copy(out=dst, in_=pj),
                     reads=[self.b_ps[6], self.b_ps[7]], writes=[b_uT[s2]])
            uT = uTs[s2]
            if _DBG_M1STOP <= 2:
                continue
            if kind != 2:
                blocks = []
                if own:
                    blocks += [(0, 0), (1, 512)]
                blocks += [(2, 1024), (3, 1536), (4, 2048), (5, 2560)]
            else:
                blocks = []
                if own:
                    blocks += [(0, 0), (1, 512)]
                blocks += [(2, 1024)]
            for (bk, c0) in blocks:
                wdt = 512 if kind != 2 or c0 < 1024 else 256
                for k in range(8):
                    P.op("pe", lambda e, bk=bk, c0=c0, k=k, wdt=wdt, uT=uT: e.matmul(
                        self.bank(bk)[:, 0:wdt], lhsT=uT[:, k, :], rhs=w_in[:, k, c0:c0 + wdt],
                        start=(k == 0), stop=(k == 7)),
                        reads=[b_uT[s2], b_w], writes=[self.b_ps[bk]])
            if _DBG_M1STOP <= 3:
                continue
            qk = qk_sb[s2]
            cs_t = self.cs[:, t, :]
            if kind != 2:
                nh_q, nh_k = 16, 16
                kb0 = 2
                if own:
                    P.op("act", lambda e, qk=qk: e.copy(out=qk[:, 0:2048], in_=self.bank(0, 4)),
                         reads=self.b_ps[0:4], writes=[b_qk[s2]])
                    heads = [(self.bank(bb).rearrange("p (h c) -> p h c", c=64),
                              qk[:, bb * 512:(bb + 1) * 512].rearrange("p (h c) -> p h c", c=64), 8,
                              [self.b_ps[bb]]) for bb in range(4)]
                else:
                    P.op("act", lambda e, qk=qk: e.copy(out=qk[:, 1024:2048], in_=self.bank(2, 2)),
                         reads=self.b_ps[2:4], writes=[b_qk[s2]])
                    heads = [(self.bank(bb).rearrange("p (h c) -> p h c", c=64),
                              qk[:, bb * 512:(bb + 1) * 512].rearrange("p (h c) -> p h c", c=64), 8,
                              [self.b_ps[bb]]) for bb in (2, 3)]
            else:
                heads = []
                if own:
                    P.op("act", lambda e, qk=qk: e.copy(out=qk[:, 0:1024], in_=self.bank(0, 2)),
                         reads=self.b_ps[0:2], writes=[b_qk[s2]])
                    for bb in range(2):
                        heads.append((self.bank(bb).rearrange("p (h c) -> p h c", c=64),
                                      qk[:, bb * 512:(bb + 1) * 512].rearrange("p (h c) -> p h c", c=64), 8,
                                      [self.b_ps[bb]]))
                kd = qk[:, 1024:1280].rearrange("p (g r c) -> p g r c", g=2, r=2)
                ksrc = self.bank(2)[:, 0:128].rearrange("p (g c) -> p g c", g=2)
                for r in range(2):
                    P.op("act", lambda e, r=r, kd=kd, ksrc=ksrc: e.copy(out=kd[:, :, r, :], in_=ksrc),
                         reads=[self.b_ps[2]], writes=[b_qk[s2]])
            if kind != 2:
                ropes = heads
            else:
                ropes = heads
            rtt = rt[0]
            if "rope" in _DBG_SKIP:
                ropes = []
            for (src3, dst3, nh, bsrc) in ropes:
                c_b = cs_t[:, 0:8].unsqueeze(1).to_broadcast([128, nh, 8])
                s_b = cs_t[:, 8:16].unsqueeze(1).to_broadcast([128, nh, 8])
                t1 = src3[:, :, 0:8]
                t2 = src3[:, :, 8:16]
                rr = [rtt[:, a, 0:nh, :] for a in range(4)]
                P.op("dve", lambda e, t1=t1, c_b=c_b, rr=rr: e.tensor_mul(out=rr[0], in0=t1, in1=c_b),
                     reads=bsrc + [self.b_cs], writes=[b_rt])
                P.op("dve", lambda e, t2=t2, s_b=s_b, rr=rr: e.tensor_mul(out=rr[1], in0=t2, in1=s_b),
                     reads=bsrc + [self.b_cs], writes=[b_rt])
                P.op("dve", lambda e, t2=t2, c_b=c_b, rr=rr: e.tensor_mul(out=rr[2], in0=t2, in1=c_b),
                     reads=bsrc + [self.b_cs], writes=[b_rt])
                P.op("dve", lambda e, t1=t1, s_b=s_b, rr=rr: e.tensor_mul(out=rr[3], in0=t1, in1=s_b),
                     reads=bsrc + [self.b_cs], writes=[b_rt])
                P.op("dve", lambda e, dst3=dst3, rr=rr: e.tensor_sub(out=dst3[:, :, 0:8], in0=rr[0], in1=rr[1]),
                     reads=[b_rt], writes=[b_qk[s2]])
                P.op("dve", lambda e, dst3=dst3, rr=rr: e.tensor_add(out=dst3[:, :, 8:16], in0=rr[2], in1=rr[3]),
                     reads=[b_rt], writes=[b_qk[s2]])
            if kind == 2:
                src3 = self.bank(2)[:, 0:128].rearrange("p (g c) -> p g c", g=2)
                kd4 = qk[:, 1024:1280].rearrange("p (g r c) -> p g r c", g=2, r=2)
                c_b = cs_t[:, 0:8].unsqueeze(1).to_broadcast([128, 2, 8])
                s_b = cs_t[:, 8:16].unsqueeze(1).to_broadcast([128, 2, 8])
                t1 = src3[:, :, 0:8]
                t2 = src3[:, :, 8:16]
                rr = [rtt[:, a, 0:2, :] for a in range(4)]
                bsrc = [self.b_ps[2]]
                P.op("dve", lambda e, t1=t1, c_b=c_b, rr=rr: e.tensor_mul(out=rr[0], in0=t1, in1=c_b),
                     reads=bsrc + [self.b_cs], writes=[b_rt])
                P.op("dve", lambda e, t2=t2, s_b=s_b, rr=rr: e.tensor_mul(out=rr[1], in0=t2, in1=s_b),
                     reads=bsrc + [self.b_cs], writes=[b_rt])
                P.op("dve", lambda e, t2=t2, c_b=c_b, rr=rr: e.tensor_mul(out=rr[2], in0=t2, in1=c_b),
                     reads=bsrc + [self.b_cs], writes=[b_rt])
                P.op("dve", lambda e, t1=t1, s_b=s_b, rr=rr: e.tensor_mul(out=rr[3], in0=t1, in1=s_b),
                     reads=bsrc + [self.b_cs], writes=[b_rt])
                for r in range(2):
                    P.op("dve", lambda e, r=r, kd4=kd4, rr=rr: e.tensor_sub(
                        out=kd4[:, :, r, 0:8], in0=rr[0], in1=rr[1]), reads=[b_rt], writes=[b_qk[s2]])
                    P.op("dve", lambda e, r=r, kd4=kd4, rr=rr: e.tensor_add(
                        out=kd4[:, :, r, 8:16], in0=rr[2], in1=rr[3]), reads=[b_rt], writes=[b_qk[s2]])
            if _DBG_M1STOP <= 4:
                continue
            vs = v_sb[s2]
            if kind != 2:
                for hb in range(2):
                    P.op("dve", lambda e, vs=vs, hb=hb: e.tensor_copy(out=vs[:, hb * 512:(hb + 1) * 512],
                                                                     in_=self.bank(4 + hb)),
                         reads=self.b_ps[4:6], writes=[b_v[s2]])
                P.dma("sp", lambda e, vs=vs, t=t: e.dma_start(out=self.v_d[t * 128:(t + 1) * 128, :], in_=vs),
                      b_v[s2], writes=[self.b_vd[t]])
            else:
                P.op("dve", lambda e, vs=vs: e.tensor_copy(out=vs[:, 0:128], in_=self.bank(2)[:, 128:256]),
                     reads=[self.b_ps[2]], writes=[b_v[s2]])
                P.dma("sp", lambda e, vs=vs, t=t: e.dma_start(out=self.v_d[t * 128:(t + 1) * 128, 0:128],
                                                             in_=vs[:, 0:128]),
                      b_v[s2], writes=[self.b_vd[t]])
            if _DBG_M1STOP <= 5:
                continue
            g4 = t // 4
            x4 = t % 4
            sg = g4 % 2
            if own:
                pb = self.bank_bf(6 + s2).rearrange("p (k t) -> p k t", k=8)
                for c in range(8):
                    P.op("pe", lambda e, c=c, pb=pb, qk=qk: e.transpose(
                        out=pb[:, c, :], in_=qk[:, c * 128:(c + 1) * 128], identity=self.ident[:]),
                        reads=[b_qk[s2], self.b_const], writes=[self.b_ps[6 + s2]])
                P.op("act", lambda e, pb=pb, sg=sg, x4=x4: e.copy(
                    out=qT_st[sg][:, :, x4 * 128:(x4 + 1) * 128], in_=pb),
                    reads=[self.b_ps[6 + s2]], writes=[b_qT[sg]])
            pb = self.bank_bf(6 + s2).rearrange("p (k t) -> p k t", k=8)
            koff = 1024
            for c in range(nkc if kind != 2 else 2):
                P.op("pe", lambda e, c=c, pb=pb, qk=qk: e.transpose(
                    out=pb[:, c, :], in_=qk[:, koff + c * 128:koff + (c + 1) * 128], identity=self.ident[:]),
                    reads=[b_qk[s2], self.b_const], writes=[self.b_ps[6 + s2]])
            nkt = nkc if kind != 2 else 2
            P.op("dve", lambda e, pb=pb, sg=sg, x4=x4, nkt=nkt: e.tensor_copy(
                out=kT_st[sg][:, 0:nkt, x4 * 128:(x4 + 1) * 128], in_=pb[:, 0:nkt, :]),
                reads=[self.b_ps[6 + s2]], writes=[b_kT[sg]])
            if _DBG_M1STOP <= 6:
                continue
            last_in_group = (x4 == 3) or (it == len(tiles) - 1)
            if last_in_group:
                ntk = (x4 + 1) * 128
                if own:
                    P.dma("sp", lambda e, sg=sg, g4=g4: e.dma_start(
                        out=self.qT_d[:, :, g4 * 512:(g4 + 1) * 512].rearrange("c p t -> p c t"),
                        in_=qT_st[sg]), b_qT[sg], writes=[self.b_qTd[g4]])
                P.dma("sp", lambda e, sg=sg, g4=g4, ntk=ntk, nkt=nkt: e.dma_start(
                    out=self.kT_d[0:nkt, :, g4 * 512:g4 * 512 + ntk].rearrange("c p t -> p c t"),
                    in_=kT_st[sg][:, 0:nkt, 0:ntk]), b_kT[sg], writes=[self.b_kTd[g4]])
        P.barrier()

    def stage_m2_banded(self, l):
        P = self.P
        kind = layer_kind(l)
        w = self.W[l]
        self.areset()
        OT = self.abf(8 * HALF).rearrange("p (c t) -> p c t", c=8)
        self.OT = OT
        self.b_OT = P.buf("OT")
        if kind == 0:
            dils = A_DILS
            R = 64
            PADK = 64 * 16
            NKTOK = HALF + 1024
        else:
            dils = (1,)
            R = 128
            PADK = 128
            NKTOK = HALF + 128
        nkt_per_q = 2 if kind == 0 else 3
        ntl = {d: (HALF // d + 2 * R) // 128 for d in dils}
        voff = {}
        tot = 0
        for d in dils:
            voff[d] = tot
            tot += d * ntl[d]
        NB = 2
        QTs = [self.abf(HALF)] * NB
        KTs = [self.abf(PADK + NKTOK) for _ in range(NB)]
        Vps = [self.abf(tot * 128).rearrange("p (j c) -> p j c", c=128) for _ in range(NB)]
        b_Q = [P.buf("Q")] * NB
        pcount = 0
        b_K = P.bufs(NB, "K")
        b_V = P.bufs(NB, "V")
        accO = self.af32(HALF)
        accD = self.af32(HALF)
        b_acc = P.buf("acc")
        NPT = 4 if kind == 0 else 6
        Pt = [self.abf(512) for _ in range(NPT)]
        b_Pt = P.bufs(NPT, "Pt")
        rec = self.af32(512)
        b_rec = P.buf("rec")
        tO = self.af32(512)
        tD = self.af32(512)
        b_tO = P.buf("tO")
        es_t = self.small[:, 32:40]
        b_es = P.buf("es")
        if kind == 2:
            sk = w["sink"]
            for hp in range(8):
                for h2 in range(2):
                    P.dma("sp", lambda e, hp=hp, h2=h2: e.dma_start(
                        out=es_t[h2 * 64:(h2 + 1) * 64, hp:hp + 1],
                        in_=sk[2 * hp + h2:2 * hp + h2 + 1, :].partition_broadcast(64)), b_es)
            P.op("act", lambda e: e.activation(out=es_t, in_=es_t, func=AF.Exp), reads=[b_es], writes=[b_es])
        for i in range(NB):
            P.op("pool", lambda e, i=i: e.memset(KTs[i][:, 0:PADK], 0.0), writes=[b_K[i]])
            P.op("pool", lambda e, i=i: e.memset(Vps[i], 0.0), writes=[b_V[i]])
        scale = 1.0 / 8.0
        for hp in range(8):
            sl = hp % NB
            QT, KT, Vp = QTs[sl], KTs[sl], Vps[sl]
            if kind == 0:
                P.dma("sp", lambda e, hp=hp, KT=KT: e.dma_start(out=KT[:, PADK:PADK + NKTOK],
                                                               in_=self.kT_d[hp][:, 0:NKTOK]),
                      b_K[sl], reads=self.b_kTd)
            else:
                kv = hp // 4
                P.dma("sp", lambda e, kv=kv, KT=KT: e.dma_start(out=KT[:, PADK:PADK + NKTOK],
                                                               in_=self.kT_d[kv][:, 0:NKTOK]),
                      b_K[sl], reads=self.b_kTd)
            for d in dils:
                for r in range(d):
                    base_t = voff[d] + r * ntl[d]
                    n1 = ntl[d] - 1
                    if kind == 0:
                        cols = slice(hp * 128, (hp + 1) * 128)
                        src = self.v_d[ssl(R * d + r, n1 * 128, d), cols]
                        P.dma("sp", lambda e, src=src, Vp=Vp, base_t=base_t, n1=n1: e.dma_start(
                            out=Vp[:, base_t + 1:base_t + 1 + n1, :],
                            in_=src.rearrange("(j i) c -> i j c", i=128)), b_V[sl], reads=self.b_vd)
                        src0 = self.v_d[ssl(r, 128 - R, d), cols]
                        P.dma("sp", lambda e, src0=src0, Vp=Vp, base_t=base_t: e.dma_start(
                            out=Vp[R:128, base_t, :], in_=src0), b_V[sl], reads=self.b_vd)
                    else:
                        kv = hp // 4
                        for dup in range(2):
                            src = self.v_d[0:n1 * 128, kv * 64:(kv + 1) * 64]
                            P.dma("sp", lambda e, src=src, Vp=Vp, base_t=base_t, n1=n1, dup=dup: e.dma_start(
                                out=Vp[:, base_t + 1:base_t + 1 + n1, dup * 64:(dup + 1) * 64],
                                in_=src.rearrange("(j i) c -> i j c", i=128)), b_V[sl], reads=self.b_vd)
            P.dma("sp", lambda e, hp=hp, QT=QT: e.dma_start(out=QT, in_=self.qT_d[hp]), b_Q[sl],
                  reads=self.b_qTd)
            if _DBG_M2STOP <= 0:
                continue
            groups = [(h2, pi, d, qg) for h2 in range(2) for pi, d in enumerate(dils) for qg in range(8)]
            sched = []
            for gi in range(len(groups)):
                if gi == 0:
                    sched.append((0, 0))
                if gi + 1 < len(groups):
                    sched.append((gi + 1, 0))
                sched.append((gi, 1))
            if _DBG_M2STOP <= 0:
                sched = []
            for gidx, which in sched:
                for (h2, pi, d, qg) in (groups[gidx],):
                    ph = slice(h2 * 64, (h2 + 1) * 64)
                    if True:
                        nq = 32 // d
                        qis = [qg * 4 + x for x in range(4)]
                        rj = [divmod(qi, nq) for qi in qis]
                        bO, bD = 4 + 2 * (gidx % 2), 5 + 2 * (gidx % 2)
                        for phase, kt in [(which, k_) for k_ in range(nkt_per_q)]:
                            sb_i = (gidx * nkt_per_q + kt) % 4
                            skip = [False] * 4
                            for x, (r, j) in enumerate(rj):
                                jp = j + kt
                                if kind == 2 and jp == 0:
                                    skip[x] = True
                            pt_i = (gidx * nkt_per_q + kt) % NPT
                            PT = Pt[pt_i]
                            if phase == 1:
                                for x, (r, j) in enumerate(rj):
                                    jp = j + kt
                                    vt = voff[d] + r * ntl[d] + jp
                                    P.op("pe", lambda e, x=x, vt=vt, Vp=Vp, PT=PT, kt=kt, bO=bO: e.matmul(
                                        self.bank(bO)[:, x * 128:(x + 1) * 128], lhsT=Vp[:, vt, :],
                                        rhs=PT[:, x * 128:(x + 1) * 128],
                                        start=(kt == 0 and x == 0), stop=(kt == nkt_per_q - 1)),
                                        reads=[b_V[sl], b_Pt[pt_i]], writes=[self.b_ps[bO]])
                                P.op("pe", lambda e, PT=PT, kt=kt, bD=bD: e.matmul(
                                    self.bank(bD), lhsT=self.ones_bf[:], rhs=PT,
                                    start=(kt == 0), stop=(kt == nkt_per_q - 1)),
                                    reads=[self.b_const, b_Pt[pt_i]], writes=[self.b_ps[bD]])
                                continue
                            for x, (r, j) in enumerate(rj):
                                jp = j + kt
                                kc0 = PADK + (jp * 128 - R) * d + r
                                qc0 = j * 128 * d + r
                                P.op("pe", lambda e, x=x, kc0=kc0, qc0=qc0, d=d, KT=KT, QT=QT, ph=ph, sb_i=sb_i: e.matmul(
                                    self.bank(sb_i)[:, x * 128:(x + 1) * 128],
                                    lhsT=KT[ph, ssl(kc0, 128, d)], rhs=QT[ph, ssl(qc0, 128, d)],
                                    start=True, stop=True),
                                    reads=[b_K[sl], b_Q[sl]], writes=[self.b_ps[sb_i]])
                            P.op("act", lambda e, PT=PT, sb_i=sb_i: e.activation(
                                out=PT, in_=self.bank(sb_i), func=AF.Exp, scale=scale),
                                reads=[self.b_ps[sb_i]], writes=[b_Pt[pt_i]])
                            if _DBG_M2STOP <= 1:
                                continue
                            PT3 = PT.rearrange("p (x q) -> p x q", x=4)
                            if kt == 0:
                                P.op("pool", lambda e, PT3=PT3: e.affine_select(
                                    out=PT3, in_=PT3, pattern=[[0, 4], [-1, 128]], compare_op=ALU.is_ge,
                                    fill=0.0, base=0, channel_multiplier=1),
                                    reads=[b_Pt[pt_i]], writes=[b_Pt[pt_i]])
                            elif kt == nkt_per_q - 1:
                                P.op("pool", lambda e, PT3=PT3: e.affine_select(
                                    out=PT3, in_=PT3, pattern=[[0, 4], [1, 128]], compare_op=ALU.is_ge,
                                    fill=0.0, base=0, channel_multiplier=-1),
                                    reads=[b_Pt[pt_i]], writes=[b_Pt[pt_i]])
                            for x, (r, j) in enumerate(rj):
                                jp = j + kt
                                if jp == 0:
                                    if kind == 0:
                                        P.op("pool", lambda e, PT3=PT3, x=x: e.affine_select(
                                            out=PT3[:, x, :], in_=PT3[:, x, :], pattern=[[0, 128]],
                                            compare_op=ALU.is_ge, fill=0.0, base=-R, channel_multiplier=1),
                                            reads=[b_Pt[pt_i]], writes=[b_Pt[pt_i]])
                                    else:
                                        P.op("pool", lambda e, PT3=PT3, x=x: e.memset(PT3[:, x, :], 0.0),
                                             writes=[b_Pt[pt_i]])
                        if which == 0 or _DBG_M2STOP <= 3:
                            continue
                        if d == 1:
                            dO = accO[ph, qg * 512:(qg + 1) * 512]
                            dD = accD[ph, qg * 512:(qg + 1) * 512]
                            sO = self.bank(bO)[ph, :]
                            sD = self.bank(bD)[ph, :]
                        elif d == 4:
                            r0, j0 = rj[0]
                            o0 = r0 + j0 * 512
                            dO = accO[ph, ssl(o0, 512, 4)].rearrange("p (x i) -> p x i", x=4)
                            dD = accD[ph, ssl(o0, 512, 4)].rearrange("p (x i) -> p x i", x=4)
                            sO = self.bank(bO)[ph, :].rearrange("p (x i) -> p x i", x=4)
                            sD = self.bank(bD)[ph, :].rearrange("p (x i) -> p x i", x=4)
                        else:
                            r0, j0 = rj[0]
                            assert j0 == 0
                            dO = [accO[ph, ssl(r_ + j_ * 2048, 128, 16)] for (r_, j_) in rj]
                            dD = [accD[ph, ssl(r_ + j_ * 2048, 128, 16)] for (r_, j_) in rj]
                            sO = [self.bank(bO)[ph, x * 128:(x + 1) * 128] for x in range(4)]
                            sD = [self.bank(bD)[ph, x * 128:(x + 1) * 128] for x in range(4)]
                        if d == 1:
                            P.op("dve", lambda e, dO=dO, sO=sO: e.tensor_copy(out=dO, in_=sO),
                                 reads=[self.b_ps[bO]], writes=[b_acc])
                            P.op("dve", lambda e, dD=dD, sD=sD: e.tensor_copy(out=dD, in_=sD),
                                 reads=[self.b_ps[bD]], writes=[b_acc])
                        elif d == 4:
                            P.op("dve", lambda e, dO=dO, sO=sO: e.tensor_add(out=dO, in0=dO, in1=sO),
                                 reads=[self.b_ps[bO]], writes=[b_acc])
                            P.op("dve", lambda e, dD=dD, sD=sD: e.tensor_add(out=dD, in0=dD, in1=sD),
                                 reads=[self.b_ps[bD]], writes=[b_acc])
                        else:
                            P.op("act", lambda e, bO=bO, ph=ph: e.copy(out=tO[ph, :], in_=self.bank(bO)[ph, :]),
                                 reads=[self.b_ps[bO]], writes=[b_tO])
                            P.op("act", lambda e, bD=bD, ph=ph: e.copy(out=tD[ph, :], in_=self.bank(bD)[ph, :]),
                                 reads=[self.b_ps[bD]], writes=[b_tO])
                            for rr in range(4):
                                P.op("pool", lambda e, a=dO[rr], rr=rr, ph=ph: e.tensor_add(
                                    out=a, in0=a, in1=tO[ph, rr * 128:(rr + 1) * 128]),
                                    reads=[b_tO], writes=[b_acc])
                                P.op("pool", lambda e, a=dD[rr], rr=rr, ph=ph: e.tensor_add(
                                    out=a, in0=a, in1=tD[ph, rr * 128:(rr + 1) * 128]),
                                    reads=[b_tO], writes=[b_acc])
            if _DBG_M2STOP <= 4:
                continue
            if kind == 2:
                P.op("dve", lambda e, hp=hp: e.tensor_scalar(out=accD, in0=accD, scalar1=es_t[:, hp:hp + 1],
                                                             scalar2=None, op0=ALU.add),
                     reads=[b_es], writes=[b_acc])
            for qb in range(8):
                cs_ = slice(qb * 512, (qb + 1) * 512)
                P.op("dve", lambda e, cs_=cs_: e.reciprocal(out=rec, in_=accD[:, cs_]), reads=[b_acc], writes=[b_rec])
                P.op("pool", lambda e, cs_=cs_, hp=hp: e.tensor_mul(out=OT[:, hp, cs_], in0=accO[:, cs_], in1=rec),
                     reads=[b_acc, b_rec], writes=[self.b_OT])
        P.barrier()

    def stage_m2_dense(self, l):
        P = self.P
        w = self.W[l]
        self.areset()
        OT = self.abf(8 * HALF).rearrange("p (c t) -> p c t", c=8)
        self.OT = OT
        self.b_OT = P.buf("OT")
        NB = 2
        QTs = [self.abf(HALF) for _ in range(NB)]
        KTs = [self.abf(S) for _ in range(NB)]
        Vs = [self.abf(64 * 128).rearrange("p (j c) -> p j c", c=128) for _ in range(NB)]
        b_Q = P.bufs(NB, "Q")
        b_K = P.bufs(NB, "K")
        b_V = P.bufs(NB, "V")
        Pt = [self.abf(512) for _ in range(4)]
        b_Pt = P.bufs(4, "Pt")
        r1 = self.af32(512)
        r2 = self.af32(512)
        o1 = self.af32(512)
        o2 = self.af32(512)
        sq = self.af32(512)
        b_fin = P.buf("fin")
        lam_init = 0.8 - 0.6 * math.exp(-0.3 * l)
        lv = self.af32(256)
        b_lv = P.buf("lv")
        P.dma("sp", lambda e: e.dma_start(out=lv, in_=w["lam"].partition_broadcast(128)), b_lv)
        sm = self.small
        b_sm = P.buf("sm")
        pr = self.af32(128)
        P.op("dve", lambda e: e.tensor_mul(out=pr[:, 0:64], in0=lv[:, 0:64], in1=lv[:, 64:128]),
             reads=[b_lv], writes=[b_sm])
        P.op("dve", lambda e: e.tensor_mul(out=pr[:, 64:128], in0=lv[:, 128:192], in1=lv[:, 192:256]),
             reads=[b_lv], writes=[b_sm])
        P.op("dve", lambda e: e.reduce_sum(out=sm[:, 40:42], in_=pr.rearrange("p (a c) -> p a c", a=2), axis=AX.X),
             reads=[b_sm], writes=[b_sm])
        P.op("act", lambda e: e.activation(out=sm[:, 42:44], in_=sm[:, 40:42], func=AF.Exp), reads=[b_sm], writes=[b_sm])
        P.op("dve", lambda e: e.tensor_sub(out=sm[:, 44:45], in0=sm[:, 43:44], in1=sm[:, 42:43]),
             reads=[b_sm], writes=[b_sm])
        P.op("dve", lambda e: e.tensor_scalar(out=sm[:, 44:45], in0=sm[:, 44:45], scalar1=-lam_init, scalar2=None,
                                               op0=ALU.add), reads=[b_sm], writes=[b_sm])
        neglam = sm[:, 44:45]
        P.dma("sp", lambda e: e.dma_start(out=sm[:, 45:46], in_=w["subg"]), b_sm)
        P.op("dve", lambda e: e.tensor_scalar(out=sm[:, 45:46], in0=sm[:, 45:46], scalar1=1.0 - lam_init,
                                               scalar2=None, op0=ALU.mult), reads=[b_sm], writes=[b_sm])
        subg = sm[:, 45:46]
        scale = 1.0 / 8.0
        for h in range(8):
            sl = h % NB
            QT, KT, V = QTs[sl], KTs[sl], Vs[sl]
            P.dma("sp", lambda e, h=h, QT=QT: e.dma_start(out=QT, in_=self.qT_d[h]), b_Q[sl], reads=self.b_qTd)
            P.dma("sp", lambda e, h=h, KT=KT: e.dma_start(out=KT, in_=self.kT_d[h]), b_K[sl], reads=self.b_kTd)
            for half in range(2):
                src = self.v_d[half * 4096:(half + 1) * 4096, h * 128:(h + 1) * 128]
                P.dma("sp", lambda e, src=src, V=V, half=half: e.dma_start(
                    out=V[:, half * 32:(half + 1) * 32, :], in_=src.rearrange("(j i) c -> i j c", i=128)),
                    b_V[sl], reads=self.b_vd)
            for qb in range(8):
                qs = slice(qb * 512, (qb + 1) * 512)

                def qk_exp(kt, qs=qs, KT=KT, QT=QT, sl=sl):
                    ks = slice(kt * 128, (kt + 1) * 128)
                    pp = kt % 2
                    for c in range(2):
                        bk = pp * 2 + c
                        ph = slice(c * 64, (c + 1) * 64)
                        P.op("pe", lambda e, bk=bk, ph=ph, ks=ks: e.matmul(
                            self.bank(bk), lhsT=KT[ph, ks], rhs=QT[ph, qs], start=True, stop=True),
                            reads=[b_K[sl], b_Q[sl]], writes=[self.b_ps[bk]])
                    for c in range(2):
                        bk = pp * 2 + c
                        P.op("act", lambda e, bk=bk: e.activation(out=Pt[bk], in_=self.bank(bk), func=AF.Exp,
                                                                   scale=scale),
                             reads=[self.b_ps[bk]], writes=[b_Pt[bk]])

                def pv(kt, V=V, sl=sl):
                    pp = kt % 2
                    for c in range(2):
                        bk = pp * 2 + c
                        P.op("pe", lambda e, bk=bk, c=c, kt=kt: e.matmul(
                            self.bank(4 + c), lhsT=V[:, kt, :], rhs=Pt[bk], start=(kt == 0), stop=(kt == 63)),
                            reads=[b_V[sl], b_Pt[bk]], writes=[self.b_ps[4 + c]])
                        P.op("pe", lambda e, bk=bk, c=c, kt=kt: e.matmul(
                            self.bank(6 + c), lhsT=self.ones_bf[:], rhs=Pt[bk], start=(kt == 0), stop=(kt == 63)),
                            reads=[self.b_const, b_Pt[bk]], writes=[self.b_ps[6 + c]])

                qk_exp(0)
                for kt in range(64):
                    if kt + 1 < 64:
                        qk_exp(kt + 1)
                    pv(kt)
                P.op("dve", lambda e: e.reciprocal(out=r1, in_=self.bank(6)), reads=[self.b_ps[6]], writes=[b_fin])
                P.op("dve", lambda e: e.reciprocal(out=r2, in_=self.bank(7)), reads=[self.b_ps[7]], writes=[b_fin])
                P.op("dve", lambda e: e.tensor_mul(out=o1, in0=self.bank(4), in1=r1), reads=[self.b_ps[4], b_fin],
                     writes=[b_fin])
                P.op("dve", lambda e: e.tensor_mul(out=o2, in0=self.bank(5), in1=r2), reads=[self.b_ps[5], b_fin],
                     writes=[b_fin])
                P.op("dve", lambda e: e.scalar_tensor_tensor(out=o1, in0=o2, scalar=neglam, in1=o1,
                                                              op0=ALU.mult, op1=ALU.add),
                     reads=[b_fin, b_sm], writes=[b_fin])
                P.op("act", lambda e: e.activation(out=sq, in_=o1, func=AF.Square), reads=[b_fin], writes=[b_fin])
                P.op("pe", lambda e: e.matmul(self.bank(0), lhsT=self.ones_f[:], rhs=sq, start=True, stop=True),
                     reads=[self.b_const, b_fin], writes=[self.b_ps[0]])
                P.op("act", lambda e: e.activation(out=r1, in_=self.bank(0), func=AF.Sqrt, scale=1.0 / 128, bias=EPS),
                     reads=[self.b_ps[0]], writes=[b_fin])
                P.op("dve", lambda e: e.reciprocal(out=r1, in_=r1), reads=[b_fin], writes=[b_fin])
                P.op("dve", lambda e, h=h, qs=qs: e.scalar_tensor_tensor(
                    out=OT[:, h, qs], in0=o1, scalar=subg, in1=r1, op0=ALU.mult, op1=ALU.mult),
                    reads=[b_fin, b_sm], writes=[self.b_OT])
        P.barrier()

    def stage_s1(self, l, hsrc, b_hsrc):
        P = self.P
        w = self.W[l]
        OT, b_OT = self.OT, self.b_OT
        self.apos = 8 * HALF
        if _DBG_H1:
            for c in range(8):
                P.dma("sp", lambda e, c=c: e.dma_start(out=self.ot_d[:, c, :], in_=OT[:, c, :]), b_OT)
        w_out = self.abf(8 * D).rearrange("p (k n) -> p k n", k=8)
        wq = self.abf(8 * D).rearrange("p (k n) -> p k n", k=8)
        wo = self.abf(8 * D).rearrange("p (k n) -> p k n", k=8)
        b_wout, b_wq, b_wo = P.buf("wout"), P.buf("wq"), P.buf("wo")
        self.load_w(w_out, w["w_out"], b_wout, 8)
        self.load_w(wq, w["wq"], b_wq, 8)
        self.load_w(wo, w["wo"], b_wo, 8)
        g_post = self.af32(D)
        g_mpre = self.af32(D)
        g_mpost = self.af32(D)
        g_kv = self.af32(D)
        b_g = P.bufs(4, "g")
        self.load_g(g_post, w["g"], 1, b_g[0])
        self.load_g(g_mpre, w["g"], 2, b_g[1])
        self.load_g(g_kv, w["g"], 3, b_g[2])
        self.load_g(g_mpost, w["g"], 4, b_g[3])
        KTm = self.abf(8 * NMEM).rearrange("p (c t) -> p c t", c=8)
        Vm = self.abf(2 * D).rearrange("p (j c) -> p j c", j=2)
        b_KTm, b_Vm = P.buf("KTm"), P.buf("Vm")
        hres = [self.af32(D) for _ in range(4)]
        b_h = P.bufs(4, "h")
        tmp = self.af32(D)
        b_tmp = P.buf("tmp")
        u2 = [self.abf(D) for _ in range(2)]
        b_u2 = P.bufs(2, "u2")
        sm = self.small
        b_ss = P.bufs(4, "ss")
        mark = self.apos
        wkv = self.abf(8 * 2 * D).rearrange("p (k n) -> p k n", k=8)
        b_wkv = P.buf("wkv")
        self.load_w(wkv, w["wkv"], b_wkv, 8)
        uTm = self.abf(8 * NMEM).rearrange("p (k t) -> p k t", k=8)
        b_uTm = P.bufs(2, "uTm")
        for mt in range(2):
            ht = hres[mt]
            P.dma("sp", lambda e, mt=mt, ht=ht: e.dma_start(out=ht, in_=self.mem[mt * 128:(mt + 1) * 128, :]), b_h[mt])
            self.rms_u(ht, b_h[mt], g_kv, b_g[2], u2[mt], b_u2[mt], sm[:, 2 * mt:2 * mt + 1],
                       sm[:, 2 * mt + 1:2 * mt + 2], b_ss[mt], tmp.bitcast(BF16)[:, 0:D], b_tmp)
            self.transpose8(u2[mt], b_u2[mt], 2, uTm[:, :, mt * 128:(mt + 1) * 128], b_uTm[mt])
        for c in range(8):
            bk = 3 + c % 2
            for k in range(8):
                P.op("pe", lambda e, c=c, k=k, bk=bk: e.matmul(
                    self.bank(bk)[:, 0:NMEM], lhsT=wkv[:, k, c * 128:(c + 1) * 128], rhs=uTm[:, k, :],
                    start=(k == 0), stop=(k == 7)), reads=[b_wkv] + b_uTm, writes=[self.b_ps[bk]])
            P.op("act", lambda e, c=c, bk=bk: e.copy(out=KTm[:, c, :], in_=self.bank(bk)[:, 0:NMEM]),
                 reads=[self.b_ps[bk]], writes=[b_KTm])
        for mt in range(2):
            for cb in range(2):
                for k in range(8):
                    P.op("pe", lambda e, mt=mt, cb=cb, k=k: e.matmul(
                        self.bank(cb), lhsT=uTm[:, k, mt * 128:(mt + 1) * 128],
                        rhs=wkv[:, k, D + cb * 512:D + (cb + 1) * 512], start=(k == 0), stop=(k == 7)),
                        reads=[b_wkv] + b_uTm, writes=[self.b_ps[cb]])
            P.op("act", lambda e, mt=mt: e.copy(out=Vm[:, mt, :], in_=self.bank(0, 2)),
                 reads=self.b_ps[0:2], writes=[b_Vm])
        P.barrier()
        self.apos = mark
        u2T = self.abf(8 * 512).rearrange("p (k t) -> p k t", k=8)
        b_u2T = P.bufs(4, "u2T")
        qTx = self.abf(8 * 512).rearrange("p (k t) -> p k t", k=8)
        b_qTx = P.bufs(8, "qTx")
        oxT = self.abf(8 * 512).rearrange("p (k t) -> p k t", k=8)
        b_oxT = P.bufs(8, "oxT")
        Px = [self.abf(512) for _ in range(2)]
        b_Px = P.bufs(2, "Px")
        rec = self.af32(512)
        b_rec = P.buf("rec")
        xscale = 1.0 / 16.0
        self.b_h1d = P.fresh(32)
        for blk in range(8):
            pend_t = []
            for tt in range(4):
                t = blk * 4 + tt
                ts_ = slice(t * 128, (t + 1) * 128)
                ht = hres[tt]
                P.dma("sp", lambda e, ts_=ts_, ht=ht: e.dma_start(out=ht, in_=hsrc[ts_, :]), b_h[tt],
                      reads=[b_hsrc[min(t, len(b_hsrc) - 1)]])
                yb = 0 if tt % 2 == 0 else 3
                for cb in range(2):
                    for k in range(8):
                        P.op("pe", lambda e, cb=cb, k=k, ts_=ts_, yb=yb: e.matmul(
                            self.bank(yb + cb), lhsT=OT[:, k, ts_], rhs=w_out[:, k, cb * 512:(cb + 1) * 512],
                            start=(k == 0), stop=(k == 7)), reads=[b_OT, b_wout], writes=[self.b_ps[yb + cb]])
                ss = sm[:, 2 * tt:2 * tt + 1]
                rstd = sm[:, 2 * tt + 1:2 * tt + 2]
                self.post_norm_residual(yb, ht, b_h[tt], g_post, b_g[0], ss, rstd, b_ss[tt], tmp, b_tmp)
                s2 = tt % 2
                self.rms_u(ht, b_h[tt], g_mpre, b_g[1], u2[s2], b_u2[s2], ss, rstd, b_ss[tt],
                           tmp.bitcast(BF16)[:, 0:D], b_tmp)
                pend_t.append(lambda s2=s2, tt=tt: self.transpose8(
                    u2[s2], b_u2[s2], 2, u2T[:, :, tt * 128:(tt + 1) * 128], b_u2T[tt], evac="dve"))
                if len(pend_t) == 2:
                    pend_t.pop(0)()
            while pend_t:
                pend_t.pop(0)()
            for c in range(8):
                bk = 3 + c % 2
                for k in range(8):
                    P.op("pe", lambda e, c=c, k=k, bk=bk: e.matmul(
                        self.bank(bk), lhsT=wq[:, k, c * 128:(c + 1) * 128], rhs=u2T[:, k, :],
                        start=(k == 0), stop=(k == 7)), reads=[b_wq] + b_u2T, writes=[self.b_ps[bk]])
                P.op("act", lambda e, c=c, bk=bk: e.copy(out=qTx[:, c, :], in_=self.bank(bk)),
                     reads=[self.b_ps[bk]], writes=[b_qTx[c]])
            for hx in range(4):
                for mt in range(2):
                    bk = 3 + mt
                    for dc in range(2):
                        c = hx * 2 + dc
                        P.op("pe", lambda e, c=c, mt=mt, dc=dc, bk=bk: e.matmul(
                            self.bank(bk), lhsT=KTm[:, c, mt * 128:(mt + 1) * 128], rhs=qTx[:, c, :],
                            start=(dc == 0), stop=(dc == 1)), reads=[b_KTm, b_qTx[c]], writes=[self.b_ps[bk]])
                    P.op("act", lambda e, mt=mt, bk=bk: e.activation(out=Px[mt], in_=self.bank(bk), func=AF.Exp,
                                                                      scale=xscale),
                         reads=[self.b_ps[bk]], writes=[b_Px[mt]])
                for mt in range(2):
                    for dc in range(2):
                        c = hx * 2 + dc
                        P.op("pe", lambda e, c=c, mt=mt, dc=dc: e.matmul(
                            self.bank(5 + dc), lhsT=Vm[:, mt, c * 128:(c + 1) * 128], rhs=Px[mt],
                            start=(mt == 0), stop=(mt == 1)), reads=[b_Vm, b_Px[mt]], writes=[self.b_ps[5 + dc]])
                    P.op("pe", lambda e, mt=mt: e.matmul(
                        self.bank(7), lhsT=self.ones_bf[:], rhs=Px[mt], start=(mt == 0), stop=(mt == 1)),
                        reads=[self.b_const, b_Px[mt]], writes=[self.b_ps[7]])
                P.op("dve", lambda e: e.reciprocal(out=rec, in_=self.bank(7)), reads=[self.b_ps[7]], writes=[b_rec])
                for dc in range(2):
                    c = hx * 2 + dc
                    P.op("dve", lambda e, c=c, dc=dc: e.tensor_mul(out=oxT[:, c, :], in0=self.bank(5 + dc), in1=rec),
                         reads=[self.b_ps[5 + dc], b_rec], writes=[b_oxT[c]])
            for tt in range(4):
                t = blk * 4 + tt
                ht = hres[tt]
                yb = 0 if tt % 2 == 0 else 3
                for cb in range(2):
                    for c in range(8):
                        P.op("pe", lambda e, cb=cb, c=c, tt=tt, yb=yb: e.matmul(
                            self.bank(yb + cb), lhsT=oxT[:, c, tt * 128:(tt + 1) * 128],
                            rhs=wo[:, c, cb * 512:(cb + 1) * 512], start=(c == 0), stop=(c == 7)),
                            reads=[b_oxT[c], b_wo], writes=[self.b_ps[yb + cb]])
                ss = sm[:, 2 * tt:2 * tt + 1]
                rstd = sm[:, 2 * tt + 1:2 * tt + 2]
                self.post_norm_residual(yb, ht, b_h[tt], g_mpost, b_g[3], ss, rstd, b_ss[tt], tmp, b_tmp)
                P.dma("sp", lambda e, t=t, ht=ht: e.dma_start(out=self.h1_d[t * 128:(t + 1) * 128, :], in_=ht),
                      b_h[tt], writes=[self.b_h1d[t]])
        P.barrier()

    def stage_s2(self, l, hdst, b_hdst):
        P = self.P
        w = self.W[l]
        self.areset()
        NF = DFF // 128
        wgu = self.abf(8 * 2 * DFF).rearrange("p (k n) -> p k n", k=8)
        wd = self.abf(NF * D).rearrange("p (k n) -> p k n", k=NF)
        b_wgu, b_wd = P.buf("wgu"), P.buf("wd")
        self.load_w(wgu, w["wgu"], b_wgu, 8)
        self.load_w(wd, w["wd"], b_wd, NF)
        g_pre = self.af32(D)
        g_post = self.af32(D)
        b_g = P.bufs(2, "g")
        self.load_g(g_pre, w["g"], 5, b_g[0])
        self.load_g(g_post, w["g"], 6, b_g[1])
        hres = [self.af32(D) for _ in range(4)]
        b_h = P.bufs(4, "h")
        tmp = self.af32(D)
        b_tmp = P.buf("tmp")
        u3 = [self.abf(D) for _ in range(2)]
        b_u3 = P.bufs(2, "u3")
        u3T = self.abf(8 * 512).rearrange("p (k t) -> p k t", k=8)
        b_u3T = P.bufs(4, "u3T")
        hT = self.abf(NF * 512).rearrange("p (k t) -> p k t", k=NF)
        b_hT = P.bufs(NF, "hT")
        sg = [self.af32(512) for _ in range(2)]
        b_sg = P.bufs(2, "sg")
        sm = self.small
        b_ss = P.bufs(4, "ss")
        for blk in range(8):
            for tt in range(4):
                t = blk * 4 + tt
                ht = hres[tt]
                P.dma("sp", lambda e, t=t, ht=ht: e.dma_start(out=ht, in_=self.h1_d[t * 128:(t + 1) * 128, :]),
                      b_h[tt], reads=[self.b_h1d[t]])
                ss = sm[:, 2 * tt:2 * tt + 1]
                rstd = sm[:, 2 * tt + 1:2 * tt + 2]
                s2 = tt % 2
                self.rms_u(ht, b_h[tt], g_pre, b_g[0], u3[s2], b_u3[s2], ss, rstd, b_ss[tt],
                           tmp.bitcast(BF16)[:, 0:D], b_tmp)
                self.transpose8(u3[s2], b_u3[s2], 2, u3T[:, :, tt * 128:(tt + 1) * 128], b_u3T[tt], evac="dve")
            for _w in range(_WARM_S2):
                P.op("pe", lambda e: e.matmul(self.bank(7), lhsT=self.ones_bf[:], rhs=wgu[:, 0, 0:512],
                                              start=True, stop=True), reads=[b_wgu], writes=[self.b_ps[7]])
            for f in range(NF):
                pp = f % 2
                bg, bu = 3 + pp, 5 + pp
                for k in range(8):
                    P.op("pe", lambda e, f=f, k=k, bg=bg: e.matmul(
                        self.bank(bg), lhsT=wgu[:, k, f * 128:(f + 1) * 128], rhs=u3T[:, k, :],
                        start=(k == 0), stop=(k == 7)), reads=[b_wgu] + b_u3T, writes=[self.b_ps[bg]])
                for k in range(8):
                    P.op("pe", lambda e, f=f, k=k, bu=bu: e.matmul(
                        self.bank(bu), lhsT=wgu[:, k, DFF + f * 128:DFF + (f + 1) * 128], rhs=u3T[:, k, :],
                        start=(k == 0), stop=(k == 7)), reads=[b_wgu] + b_u3T, writes=[self.b_ps[bu]])
                P.op("act", lambda e, pp=pp, bg=bg: e.activation(out=sg[pp], in_=self.bank(bg), func=AF.Silu),
                     reads=[self.b_ps[bg]], writes=[b_sg[pp]])
                P.op("dve", lambda e, pp=pp, bu=bu, f=f: e.tensor_mul(out=hT[:, f, :], in0=self.bank(bu), in1=sg[pp]),
                     reads=[self.b_ps[bu], b_sg[pp]], writes=[b_hT[f]])
            for tt in range(4):
                t = blk * 4 + tt
                ht = hres[tt]
                yb = 0 if tt % 2 == 0 else 3
                for cb in range(2):
                    for f in range(NF):
                        P.op("pe", lambda e, cb=cb, f=f, tt=tt, yb=yb: e.matmul(
                            self.bank(yb + cb), lhsT=hT[:, f, tt * 128:(tt + 1) * 128],
                            rhs=wd[:, f, cb * 512:(cb + 1) * 512], start=(f == 0), stop=(f == NF - 1)),
                            reads=[b_hT[f], b_wd], writes=[self.b_ps[yb + cb]])
                ss = sm[:, 2 * tt:2 * tt + 1]
                rstd = sm[:, 2 * tt + 1:2 * tt + 2]
                self.post_norm_residual(yb, ht, b_h[tt], g_post, b_g[1], ss, rstd, b_ss[tt], tmp, b_tmp)
                P.dma("sp", lambda e, t=t, ht=ht: e.dma_start(out=hdst[t * 128:(t + 1) * 128, :], in_=ht),
                      b_h[tt], writes=[b_hdst[t]])
                self.final_bufs = b_h
        P.barrier()

    def build(self):
        P = self.P
        self.alloc()
        self.consts()
        b_hin = [P.buf("hin")]
        self.final_bufs = []
        st = _DBG_STAGES
        hcur, b_hcur = self.hin, b_hin
        gath, b_gath = None, None
        for li, l in enumerate(self.layers):
            last = li == len(self.layers) - 1
            hdst = self.hout if last else self.hx_d[li % 2]
            b_hdst = P.fresh(32, "hdst")
            if "m1" in st:
                self.stage_m1(l, hcur, b_hcur, gath, b_gath)
            if "m2" in st:
                if layer_kind(l) == 1:
                    self.stage_m2_dense(l)
                else:
                    self.stage_m2_banded(l)
            if "s1" in st:
                self.stage_s1(l, hcur, b_hcur)
            if "s2" in st:
                self.stage_s2(l, hdst, b_hdst)
            if not last:
                ln = self.layers[li + 1]
                n = {0: 8, 1: 32, 2: 4}[layer_kind(ln)] * 128
                gath = self.G_d[ln]
                b_gath = Buf("G")
                cr = min(n, 512)
                for ch in range(n // cr):
                    r0 = HALF - n + ch * cr
                    P.dma("pool", lambda e, hdst=hdst, r0=r0, cr=cr, ch=ch, gath=gath: e.collective_compute(
                        "AllGather", ALU.bypass, replica_groups=[[0, 1], [2, 3], [4, 5], [6, 7]],
                        ins=[hdst[r0:r0 + cr, :].opt()], outs=[gath[ch * 2 * cr:(ch + 1) * 2 * cr, :].opt()]),
                        P.buf("cc"), reads=b_hdst, writes=[b_gath], inc=1)
                P.barrier()
                hcur, b_hcur = hdst, b_hdst
        n = P.finish(self.final_bufs)
        self.es.close()
        return n


_PROG_CACHE = {}
_WARM_S2 = 0
_DBG_M1T = None
_DBG_H1 = False
_DBG_M2STOP = 99
_DBG_M1STOP = 99
_DBG_SKIP = ()
_DBG_STAGES = ("m1", "m2", "s1", "s2")


def _get_prog(layers):
    key = tuple(layers)
    if key not in _PROG_CACHE:
        b = Builder(list(layers), 0)
        b.build()
        _PROG_CACHE[key] = (b, list(layers))
    return _PROG_CACHE[key]


def _layer_inputs(l, lname, p):
    k, j = l % 3, l // 3
    g = np.stack([p["mix_pre_g"][l], p["mix_post_g"][l], p["mem_pre_g"][l], p["mem_kv_g"][l],
                  p["mem_post_g"][l], p["ffn_pre_g"][l], p["ffn_post_g"][l]]).astype(np.float32)
    d = {f"g{lname}": g}
    if k == 0:
        d[f"w_in{lname}"] = p["a_w_in"][j]
        d[f"w_out{lname}"] = p["a_w_out"][j]
    elif k == 1:
        d[f"w_in{lname}"] = p["b_w_in"][j]
        d[f"w_out{lname}"] = p["b_w_out"][j]
        d[f"lam{lname}"] = np.concatenate([p["b_lam_q1"][j], p["b_lam_k1"][j], p["b_lam_q2"][j],
                                           p["b_lam_k2"][j]]).reshape(1, 256).astype(np.float32)
        d[f"subg{lname}"] = p["b_sub_g"][j].reshape(128, 1).astype(np.float32)
    else:
        d[f"w_in{lname}"] = p["c_w_in"][j]
        d[f"w_out{lname}"] = p["c_w_out"][j]
        d[f"sink{lname}"] = p["c_sink"][j].reshape(16, 1).astype(np.float32)
    d[f"wq{lname}"] = p["x_wq"][l]
    d[f"wkv{lname}"] = p["x_wkv"][l]
    d[f"wo{lname}"] = p["x_wo"][l]
    d[f"wgu{lname}"] = p["w_gate_up"][l]
    d[f"wd{lname}"] = p["w_down"][l]
    return {k_: np.ascontiguousarray(v, dtype=np.float32) for k_, v in d.items()}


def _local_order(a, half):
    if half == 0:
        return a
    return a[::-1]


def run_layers(layers, h, mem, positions, params):
    b, _ = _get_prog(list(layers))
    wl = {}
    for l in layers:
        wl.update(_layer_inputs(l, l, params))
    in_maps = []
    for c in range(8):
        bi, half = c // 2, c % 2
        hl = np.ascontiguousarray(_local_order(h[bi], half))
        pl = np.ascontiguousarray(_local_order(positions[bi], half)).astype(np.int32)
        sel = np.zeros((128, 2), np.float32)
        sel[:, 1 - half] = 1.0
        m = {"hin": hl, "pos": np.ascontiguousarray(pl.reshape(64, 128).T), "mem": np.ascontiguousarray(mem[bi]),
             "sel": sel}
        m.update(wl)
        in_maps.append(m)
    res = run_bass_kernel_spmd(b.nc, in_maps, core_ids=list(range(8)))
    out = np.empty_like(h)
    for c in range(8):
        bi, half = c // 2, c % 2
        o = res.results[c]["hout"]
        if half == 0:
            out[bi, :HALF] = o
        else:
            out[bi, HALF:] = o[::-1]
    return out


def kernel(**inputs):
    p = {k: np.asarray(v) for k, v in inputs.items()}
    h = np.ascontiguousarray(p["x"], dtype=np.float32)
    mem = np.ascontiguousarray(p["mem"], dtype=np.float32)
    pos = np.asarray(p["positions"]).astype(np.int32)
    return run_layers([0, 1, 2, 3], h, mem, pos, p)
```

```python
import math
import numpy as np
from contextlib import ExitStack
import concourse.bass as bass
import concourse.mybir as mybir
from concourse.bass_utils import run_bass_kernel_spmd

F32 = mybir.dt.float32
BF16 = mybir.dt.bfloat16
I32 = mybir.dt.int32
ALU = mybir.AluOpType
AF = mybir.ActivationFunctionType
AX = mybir.AxisListType

D = 1024
S = 8192
HALF = 4096
NMEM = 256
DFF = 2816
EPS = 1e-6
ROPE_THETA = 500000.0
A_DILS = (1, 4, 16)


class Buf:
    __slots__ = ("name", "last_w", "reads", "dsem", "dcount")

    def __init__(self, name):
        self.name = name
        self.last_w = None
        self.reads = []
        self.dsem = None
        self.dcount = 0


class Op:
    __slots__ = ("eng", "fn", "deps", "need", "tok", "dma", "buf", "inc")

    def __init__(self, eng, fn, deps, dma=False, buf=None, inc=16):
        self.inc = inc
        self.eng = eng
        self.fn = fn
        self.deps = deps
        self.need = False
        self.tok = None
        self.dma = dma
        self.buf = buf


class Prog:
    ENGS = ("pe", "act", "dve", "pool", "sp")
    SEM_ROT = 30000

    def __init__(self, nc, es):
        self.nc = nc
        self.es = es
        self.ops = {e: [] for e in self.ENGS}
        self.nsem = 0
        self.last = {e: None for e in self.ENGS}
        self.dma_since_bar = []
        self.named = {}

    def new_sem(self, name):
        self.nsem += 1
        return self.es.enter_context(self.nc.semaphore(f"{name}_{self.nsem}"))

    def buf(self, name=None):
        if name is None:
            return Buf("anon")
        if name not in self.named:
            self.named[name] = Buf(name)
        return self.named[name]

    def bufs(self, n, name=None):
        if name is None:
            return [Buf("anon") for _ in range(n)]
        return [self.buf(f"{name}{i}") for i in range(n)]

    def fresh(self, n, name="dram"):
        return [Buf(name) for _ in range(n)]

    @staticmethod
    def _deps(reads, writes):
        deps = []
        for b in reads:
            if b.last_w is not None:
                deps.append(b.last_w)
        for b in writes:
            if b.last_w is not None:
                deps.append(b.last_w)
            deps.extend(b.reads)
        return deps

    @staticmethod
    def _commit(op, reads, writes):
        for b in reads:
            b.reads.append(op)
        for b in writes:
            b.last_w = op
            b.reads = []

    def op(self, eng, fn, reads=(), writes=()):
        o = Op(eng, fn, self._deps(reads, writes))
        self._commit(o, reads, writes)
        self.ops[eng].append(o)
        self.last[eng] = o
        return o

    def dma(self, eng, fn, sb, reads=(), writes=(), inc=16):
        ww = [sb] + list(writes)
        o = Op(eng, fn, self._deps(list(reads), ww), dma=True, buf=sb, inc=inc)
        self._commit(o, list(reads), ww)
        self.ops[eng].append(o)
        self.dma_since_bar.append(o)
        return o

    def barrier(self):
        deps = [o for o in self.last.values() if o is not None] + list(self.dma_since_bar)
        self.dma_since_bar = []
        for e in self.ENGS:
            o = Op(e, None, list(deps))
            self.ops[e].append(o)

    def finish(self, final_bufs):
        nc = self.nc
        for e in self.ENGS:
            for o in self.ops[e]:
                for d in o.deps:
                    if d.dma:
                        continue
                    if d.eng == "pe" and o.eng == "pe" and not o.dma and o.fn is not None:
                        continue
                    d.need = True
        for e in self.ENGS:
            cnt = 0
            sem = None
            for o in self.ops[e]:
                if o.dma:
                    b = o.buf
                    if b.dsem is None:
                        b.dsem = self.new_sem("d")
                    b.dcount += o.inc
                    o.tok = (b.dsem, b.dcount)
                elif o.need:
                    if sem is None or cnt >= self.SEM_ROT:
                        sem = self.new_sem(e)
                        cnt = 0
                    cnt += 1
                    o.tok = (sem, cnt)
        engmap = {"pe": "tensor", "act": "scalar", "dve": "vector", "pool": "gpsimd", "sp": "sync"}
        final_toks = [b.last_w.tok for b in final_bufs if b.last_w is not None and b.last_w.tok is not None]

        def make_body(e):
            def body(eng):
                known = {}
                for o in self.ops[e]:
                    need = {}
                    for d in o.deps:
                        if d.tok is None:
                            continue
                        if (not d.dma) and d.eng == "pe" and e == "pe" and not o.dma and o.fn is not None:
                            continue
                        s, v = d.tok
                        if known.get(id(s), 0) >= v:
                            continue
                        cur = need.get(id(s))
                        if cur is None or cur[1] < v:
                            need[id(s)] = (s, v)
                    for s, v in need.values():
                        eng.wait_ge(s, v)
                        known[id(s)] = v
                    if o.fn is None:
                        continue
                    ins = o.fn(eng)
                    if isinstance(ins, list):
                        for i_ in ins:
                            i_.then_inc(o.tok[0], o.inc // len(ins))
                    elif o.tok is not None:
                        ins.then_inc(o.tok[0], o.inc if o.dma else 1)
                if e == "sp":
                    for s, v in final_toks:
                        if known.get(id(s), 0) >= v:
                            continue
                        eng.wait_ge(s, v)
                        known[id(s)] = v
            return body

        with nc.Block() as block:
            for e in self.ENGS:
                getattr(block, engmap[e])(make_body(e))
        return {e: len(self.ops[e]) for e in self.ENGS}


def ssl(start, n, step=1):
    return slice(start, start + (n - 1) * step + 1, step)


def layer_kind(i):
    return i % 3


def mixer_cols(kind):
    return 1280 if kind == 2 else 3072


class Builder:
    def __init__(self, layers, n_other_in):
        self.layers = layers
        nc = self.nc = bass.Bass("TRN2", target_bir_lowering=False)
        self.es = ExitStack()
        self.P = Prog(nc, self.es)
        self.din = {}
        self._io()

    def _in(self, name, shape, dt=F32):
        t = self.nc.dram_tensor(name, list(shape), dt, kind="ExternalInput").ap()
        self.din[name] = t
        return t

    def _io(self):
        nc = self.nc
        self.hin = self._in("hin", [S, D])
        self.pos = self._in("pos", [128, 64], I32)
        self.mem = self._in("mem", [NMEM, D])
        self.sel = self._in("sel", [128, 2])
        self.hout = nc.dram_tensor("hout", [HALF, D], F32, kind="ExternalOutput").ap()
        self.W = {}
        for l in self.layers:
            k = layer_kind(l)
            w = {}
            w["g"] = self._in(f"g{l}", [7, D])
            w["w_in"] = self._in(f"w_in{l}", [D, mixer_cols(k)])
            w["w_out"] = self._in(f"w_out{l}", [D, D])
            w["wq"] = self._in(f"wq{l}", [D, D])
            w["wkv"] = self._in(f"wkv{l}", [D, 2 * D])
            w["wo"] = self._in(f"wo{l}", [D, D])
            w["wgu"] = self._in(f"wgu{l}", [D, 2 * DFF])
            w["wd"] = self._in(f"wd{l}", [DFF, D])
            if k == 1:
                w["lam"] = self._in(f"lam{l}", [1, 256])
                w["subg"] = self._in(f"subg{l}", [128, 1])
            if k == 2:
                w["sink"] = self._in(f"sink{l}", [16, 1])
            self.W[l] = w
        dk = dict(kind="ExternalOutput") if _DBG_H1 else {}
        self.qT_d = nc.dram_tensor("qT_d", [8, 128, HALF], BF16, **dk).ap()
        self.kT_d = nc.dram_tensor("kT_d", [8, 128, S], BF16, **dk).ap()
        self.v_d = nc.dram_tensor("v_d", [S, D], BF16, **dk).ap()
        if _DBG_H1:
            self.ot_d = nc.dram_tensor("ot_d", [128, 8, HALF], BF16, kind="ExternalOutput").ap()
        self.h1_d = (nc.dram_tensor("h1_d", [HALF, D], F32, kind="ExternalOutput").ap() if _DBG_H1 else nc.dram_tensor("h1_d", [HALF, D], F32).ap())
        self.hx_d = [nc.dram_tensor(f"hx_d{i}", [HALF, D], F32).ap() for i in range(2)]
        self.hoth_d = nc.dram_tensor("hoth_d", [HALF, D], F32).ap()
        self.G_d = {}
        for l in self.layers[1:]:
            n = {0: 8, 1: 32, 2: 4}[layer_kind(l)] * 128
            self.G_d[l] = nc.dram_tensor(f"G_d{l}", [2 * n, D], F32).ap()

    def alloc(self):
        nc, es, P = self.nc, self.es, self.P
        self.ARENA_ELEMS = 100 * 1024
        self.arena = es.enter_context(nc.sbuf_tensor("arena", [128, self.ARENA_ELEMS], BF16))
        self.PS = es.enter_context(nc.psum_tensor("PS", [128, 4096], F32))
        self.b_ps = P.bufs(8, "ps")
        self.ident = es.enter_context(nc.sbuf_tensor("ident", [128, 128], BF16))
        self.ones_bf = es.enter_context(nc.sbuf_tensor("ones_bf", [128, 128], BF16))
        self.ones_f = es.enter_context(nc.sbuf_tensor("ones_f", [128, 128], F32))
        self.cs = es.enter_context(nc.sbuf_tensor("cs", [128, 64, 16], F32))
        self.b_const = P.buf("const")
        self.b_cs = P.buf("cs")
        self.small = es.enter_context(nc.sbuf_tensor("small", [128, 64], F32))
        self.jmat = es.enter_context(nc.sbuf_tensor("jmat", [128, 128], F32))
        self.sel_sb = es.enter_context(nc.sbuf_tensor("sel_sb", [128, 2], F32))
        self.b_sel = P.buf("sel")
        self.apos = 0

    def bank(self, i, n=1):
        return self.PS[:, i * 512:(i + n) * 512]

    def bank_bf(self, i):
        return self.PS[:, i * 512:(i + 1) * 512].bitcast(BF16)

    def areset(self):
        self.apos = 0

    def abf(self, n):
        n = (n + 15) // 16 * 16
        assert self.apos + n <= self.ARENA_ELEMS, (self.apos, n)
        ap = self.arena[:, self.apos:self.apos + n]
        self.apos += n
        return ap

    def af32(self, n):
        return self.abf(2 * n).bitcast(F32)

    def consts(self):
        P = self.P
        ident, ones_bf, ones_f = self.ident, self.ones_bf, self.ones_f
        bc = self.b_const
        P.op("pool", lambda e: e.memset(ones_bf[:], 1.0), writes=[bc])
        P.op("pool", lambda e: e.memset(ones_f[:], 1.0), writes=[bc])
        P.op("pool", lambda e: e.memset(ident[:], 1.0), writes=[bc])
        P.op("pool", lambda e: e.affine_select(out=ident[:], in_=ident[:], pattern=[[-1, 128]],
                                                compare_op=ALU.is_equal, fill=0.0, base=0,
                                                channel_multiplier=1), reads=[bc], writes=[bc])
        jm = self.jmat
        P.op("pool", lambda e: e.memset(jm[:], 1.0), writes=[bc])
        P.op("pool", lambda e: e.affine_select(out=jm[:], in_=jm[:], pattern=[[1, 128]],
                                                compare_op=ALU.is_equal, fill=0.0, base=-127,
                                                channel_multiplier=1), reads=[bc], writes=[bc])
        P.dma("sp", lambda e: e.dma_start(out=self.sel_sb[:], in_=self.sel), self.b_sel)
        cs = self.cs
        self.areset()
        posi = self.abf(128).bitcast(I32)
        posf = self.af32(64)
        tt = self.af32(64 * 16).rearrange("p (t c) -> p t c", c=16)
        ti = self.abf(2 * 64 * 16).bitcast(I32).rearrange("p (t c) -> p t c", c=16)
        tr = self.af32(64 * 16).rearrange("p (t c) -> p t c", c=16)
        b = P.buf("ropetmp")
        P.dma("sp", lambda e: e.dma_start(out=posi, in_=self.pos), b)
        P.op("dve", lambda e: e.tensor_copy(out=posf, in_=posi), reads=[b], writes=[b])
        for i in range(8):
            inv = float(np.float32(ROPE_THETA) ** np.float32(-(2.0 * i) / 16.0))
            P.op("dve", lambda e, i=i, inv=inv: e.tensor_scalar(
                out=tt[:, :, 8 + i], in0=posf, scalar1=inv, scalar2=1.0 / (2 * math.pi),
                op0=ALU.mult, op1=ALU.mult), reads=[b], writes=[b])
        P.op("dve", lambda e: e.tensor_scalar(out=tt[:, :, 0:8], in0=tt[:, :, 8:16], scalar1=0.25,
                                               scalar2=None, op0=ALU.add), reads=[b], writes=[b])
        P.op("dve", lambda e: e.tensor_copy(out=ti, in_=tt), reads=[b], writes=[b])
        P.op("dve", lambda e: e.tensor_copy(out=tr, in_=ti), reads=[b], writes=[b])
        P.op("dve", lambda e: e.tensor_sub(out=tt, in0=tt, in1=tr), reads=[b], writes=[b])
        P.op("act", lambda e: e.activation(out=cs[:], in_=tt, func=AF.Sin, scale=2 * math.pi * (1 - 1e-6)),
             reads=[b], writes=[self.b_cs])
        P.barrier()

    def load_w(self, dst, src, b, nchunk):
        P = self.P
        srcv = src.rearrange("(k p) n -> p k n", p=128)
        for k in range(nchunk):
            P.dma("pool", lambda e, k=k: e.dma_start(out=dst[:, k, :], in_=srcv[:, k, :]), b)

    def load_g(self, dst, gsrc, row, b):
        self.P.dma("sp", lambda e: e.dma_start(out=dst, in_=gsrc[row:row + 1, :].partition_broadcast(128)), b)

    def rms_u(self, src, b_src, g_bc, b_g, u_out, b_u, ss, rstd, b_s, junk, b_junk):
        P = self.P
        P.op("act", lambda e: e.activation(out=junk, in_=src, func=AF.Square, accum_out=ss),
             reads=[b_src], writes=[b_junk, b_s])
        P.op("act", lambda e: e.activation(out=rstd, in_=ss, func=AF.Sqrt, scale=1.0 / D, bias=EPS),
             reads=[b_s], writes=[b_s])
        P.op("dve", lambda e: e.reciprocal(out=rstd, in_=rstd), reads=[b_s], writes=[b_s])
        P.op("dve", lambda e: e.scalar_tensor_tensor(out=u_out, in0=src, scalar=rstd, in1=g_bc,
                                                      op0=ALU.mult, op1=ALU.mult),
             reads=[b_src, b_s, b_g], writes=[b_u])

    def transpose8(self, src, b_src, bank_i, dst, b_dst, evac="act"):
        P = self.P
        pb = self.bank_bf(bank_i).rearrange("p (k t) -> p k t", k=8)
        for k in range(8):
            P.op("pe", lambda e, k=k: e.transpose(out=pb[:, k, :], in_=src[:, k * 128:(k + 1) * 128],
                                                   identity=self.ident[:]),
                 reads=[b_src, self.b_const], writes=[self.b_ps[bank_i]])
        if evac == "act":
            P.op("act", lambda e: e.copy(out=dst, in_=pb), reads=[self.b_ps[bank_i]], writes=[b_dst])
        else:
            P.op("dve", lambda e: e.tensor_copy(out=dst, in_=pb), reads=[self.b_ps[bank_i]], writes=[b_dst])

    def post_norm_residual(self, ybank0, hres, b_h, g_bc, b_g, ss, rstd, b_s, tmp, b_tmp):
        P = self.P
        y = self.bank(ybank0, 2)
        by = [self.b_ps[ybank0], self.b_ps[ybank0 + 1]]
        P.op("act", lambda e: e.activation(out=tmp, in_=y, func=AF.Square, accum_out=ss),
             reads=by, writes=[b_tmp, b_s])
        P.op("act", lambda e: e.activation(out=rstd, in_=ss, func=AF.Sqrt, scale=1.0 / D, bias=EPS),
             reads=[b_s], writes=[b_s])
        P.op("dve", lambda e: e.reciprocal(out=rstd, in_=rstd), reads=[b_s], writes=[b_s])
        for hb in range(2):
            cs_ = slice(hb * 512, (hb + 1) * 512)
            P.op("dve", lambda e, hb=hb, cs_=cs_: e.scalar_tensor_tensor(
                out=tmp[:, cs_], in0=self.bank(ybank0 + hb), scalar=rstd, in1=g_bc[:, cs_],
                op0=ALU.mult, op1=ALU.mult), reads=by + [b_s, b_g], writes=[b_tmp])
        P.op("pool", lambda e: e.tensor_add(out=hres, in0=hres, in1=tmp), reads=[b_tmp], writes=[b_h])

    def stage_m1(self, l, hsrc, b_hsrc, gath=None, b_gath=None):
        P, nc = self.P, self.nc
        kind = layer_kind(l)
        w = self.W[l]
        ncols = mixer_cols(kind)
        n_other = {0: 8, 1: 32, 2: 1}[kind]
        if gath is not None and kind == 2:
            n_other = 4
        self.areset()
        w_in = self.abf(8 * ncols).rearrange("p (k n) -> p k n", k=8)
        b_w = P.buf("w_in")
        self.load_w(w_in, w["w_in"], b_w, 8)
        g_bc = self.af32(D)
        b_g = P.buf("g")
        self.load_g(g_bc, w["g"], 0, b_g)
        NS = 3
        hts = [self.af32(D) for _ in range(NS)]
        b_ht = P.bufs(NS, "ht")
        gt0 = self.af32(D)
        gt1 = self.af32(D)
        b_gt = P.bufs(2, "gt")
        us = [self.abf(D) for _ in range(2)]
        b_us = P.bufs(2, "u")
        uTs = [self.abf(D).rearrange("p (k t) -> p k t", k=8) for _ in range(2)]
        b_uT = P.bufs(2, "uT")
        junk = self.abf(D)
        b_junk = P.buf("junk")
        sst = self.small
        b_ss = P.bufs(NS, "ss")
        qk_sb = [self.abf(2 * D) for _ in range(2)]
        b_qk = P.bufs(2, "qk")
        v_sb = [self.abf(D) for _ in range(2)]
        b_v = P.bufs(2, "v")
        rt = [self.af32(32 * 8 * 4).rearrange("p (a h c) -> p a h c", a=4, h=32) for _ in range(1)]
        b_rt = P.buf("rt")
        qT_st = [self.abf(8 * 512).rearrange("p (c t) -> p c t", c=8) for _ in range(2)]
        b_qT = P.bufs(2, "qTst")
        kT_st = [self.abf(8 * 512).rearrange("p (c t) -> p c t", c=8) for _ in range(2)]
        b_kT = P.bufs(2, "kTst")
        self.b_qTd = P.fresh(8)
        self.b_kTd = P.fresh(16)
        self.b_vd = P.fresh(64)
        nkc = 8 if kind != 2 else 1

        b_hoth = None
        if gath is not None:
            b_hoth = P.fresh(n_other, "hoth")
            hrev = self.af32(D)
            b_hrev = P.buf("hrev")
            n_g = n_other * 128
            cr = min(n_g, 512)
            for tt in range(n_other):
                row0 = n_g - 128 * (tt + 1)
                ch, off = divmod(row0, cr)
                ra = (ch * 2) * cr + off
                rb = (ch * 2 + 1) * cr + off
                P.dma("sp", lambda e, ra=ra: e.dma_start(out=gt0, in_=gath[ra:ra + 128, :]), b_gt[0], reads=[b_gath])
                P.dma("sp", lambda e, rb=rb: e.dma_start(out=gt1, in_=gath[rb:rb + 128, :]), b_gt[1], reads=[b_gath])
                P.op("dve", lambda e: e.tensor_scalar(out=gt0, in0=gt0, scalar1=self.sel_sb[:, 0:1], scalar2=None,
                                                       op0=ALU.mult), reads=[self.b_sel], writes=[b_gt[0]])
                P.op("dve", lambda e: e.scalar_tensor_tensor(out=gt0, in0=gt1, scalar=self.sel_sb[:, 1:2],
                                                              in1=gt0, op0=ALU.mult, op1=ALU.add),
                     reads=[self.b_sel, b_gt[1]], writes=[b_gt[0]])
                for cb in range(2):
                    P.op("pe", lambda e, cb=cb: e.matmul(self.bank(cb), lhsT=self.jmat[:],
                                                         rhs=gt0[:, cb * 512:(cb + 1) * 512], start=True, stop=True),
                         reads=[b_gt[0], self.b_const], writes=[self.b_ps[cb]])
                P.op("act", lambda e: e.copy(out=hrev, in_=self.bank(0, 2)), reads=self.b_ps[0:2], writes=[b_hrev])
                P.dma("sp", lambda e, tt=tt: e.dma_start(out=self.hoth_d[tt * 128:(tt + 1) * 128, :], in_=hrev),
                      b_hrev, writes=[b_hoth[tt]])
        tiles = list(range(32)) + [32 + i for i in range(n_other)]
        if gath is not None and "norev" in _DBG_SKIP:
            tiles = list(range(32))
        if _DBG_M1T is not None:
            tiles = tiles[:_DBG_M1T]
        for it, t in enumerate(tiles):
            own = t < 32
            s3 = it % NS
            s2 = it % 2
            ht = hts[s3]
            rev = False
            if own or gath is None:
                P.dma("sp", lambda e, t=t, ht=ht: e.dma_start(out=ht, in_=hsrc[t * 128:(t + 1) * 128, :]),
                      b_ht[s3], reads=[b_hsrc[min(t, len(b_hsrc) - 1)]])
            else:
                tt = t - 32
                P.dma("sp", lambda e, tt=tt, ht=ht: e.dma_start(out=ht, in_=self.hoth_d[tt * 128:(tt + 1) * 128, :]),
                      b_ht[s3], reads=[b_hoth[tt]])
            if _DBG_M1STOP <= 0:
                continue
            ss = sst[:, 2 * s3:2 * s3 + 1]
            rstd = sst[:, 2 * s3 + 1:2 * s3 + 2]
            self.rms_u(ht, b_ht[s3], g_bc, b_g, us[s2], b_us[s2], ss, rstd, b_ss[s3], junk, b_junk)
            if _DBG_M1STOP <= 1:
                continue
            if not rev:
                self.transpose8(us[s2], b_us[s2], 6 + s2, uTs[s2], b_uT[s2], evac="act")
            else:
                pj = self.bank(6, 2).rearrange("p (k t) -> p k t", k=8)
                for k in range(8):
                    P.op("pe", lambda e, k=k, pj=pj, uu=us[s2]: e.matmul(
                        pj[:, k, :], lhsT=uu[:, k * 128:(k + 1) * 128], rhs=self.jmat[:], start=True, stop=True),
                        reads=[b_us[s2], self.b_const], writes=[self.b_ps[6], self.b_ps[7]])
                P.op("act", lambda e, pj=pj, dst=uTs[s2]: e.copy(out=dst, in_=pj),
                     reads=[self.b_ps[6], self.b_ps[7]], writes=[b_uT[s2]])
            uT = uTs[s2]
            if _DBG_M1STOP <= 2:
                continue
            if kind != 2:
                blocks = []
                if own:
                    blocks += [(0, 0), (1, 512)]
                blocks += [(2, 1024), (3, 1536), (4, 2048), (5, 2560)]
            else:
                blocks = []
                if own:
                    blocks += [(0, 0), (1, 512)]
                blocks += [(2, 1024)]
            for (bk, c0) in blocks:
                wdt = 512 if kind != 2 or c0 < 1024 else 256
                for k in range(8):
                    P.op("pe", lambda e, bk=bk, c0=c0, k=k, wdt=wdt, uT=uT: e.matmul(
                        self.bank(bk)[:, 0:wdt], lhsT=uT[:, k, :], rhs=w_in[:, k, c0:c0 + wdt],
                        start=(k == 0), stop=(k == 7)),
                        reads=[b_uT[s2], b_w], writes=[self.b_ps[bk]])
            if _DBG_M1STOP <= 3:
                continue
            qk = qk_sb[s2]
            cs_t = self.cs[:, t, :]
            if kind != 2:
                nh_q, nh_k = 16, 16
                kb0 = 2
                if own:
                    P.op("act", lambda e, qk=qk: e.copy(out=qk[:, 0:2048], in_=self.bank(0, 4)),
                         reads=self.b_ps[0:4], writes=[b_qk[s2]])
                    heads = [(self.bank(bb).rearrange("p (h c) -> p h c", c=64),
                              qk[:, bb * 512:(bb + 1) * 512].rearrange("p (h c) -> p h c", c=64), 8,
                              [self.b_ps[bb]]) for bb in range(4)]
                else:
                    P.op("act", lambda e, qk=qk: e.copy(out=qk[:, 1024:2048], in_=self.bank(2, 2)),
                         reads=self.b_ps[2:4], writes=[b_qk[s2]])
                    heads = [(self.bank(bb).rearrange("p (h c) -> p h c", c=64),
                              qk[:, bb * 512:(bb + 1) * 512].rearrange("p (h c) -> p h c", c=64), 8,
                              [self.b_ps[bb]]) for bb in (2, 3)]
            else:
                heads = []
                if own:
                    P.op("act", lambda e, qk=qk: e.copy(out=qk[:, 0:1024], in_=self.bank(0, 2)),
                         reads=self.b_ps[0:2], writes=[b_qk[s2]])
                    for bb in range(2):
                        heads.append((self.bank(bb).rearrange("p (h c) -> p h c", c=64),
                                      qk[:, bb * 512:(bb + 1) * 512].rearrange("p (h c) -> p h c", c=64), 8,
                                      [self.b_ps[bb]]))
                kd = qk[:, 1024:1280].rearrange("p (g r c) -> p g r c", g=2, r=2)
                ksrc = self.bank(2)[:, 0:128].rearrange("p (g c) -> p g c", g=2)
                for r in range(2):
                    P.op("act", lambda e, r=r, kd=kd, ksrc=ksrc: e.copy(out=kd[:, :, r, :], in_=ksrc),
                         reads=[self.b_ps[2]], writes=[b_qk[s2]])
            if kind != 2:
                ropes = heads
            else:
                ropes = heads
            rtt = rt[0]
            if "rope" in _DBG_SKIP:
                ropes = []
            for (src3, dst3, nh, bsrc) in ropes:
                c_b = cs_t[:, 0:8].unsqueeze(1).to_broadcast([128, nh, 8])
                s_b = cs_t[:, 8:16].unsqueeze(1).to_broadcast([128, nh, 8])
                t1 = src3[:, :, 0:8]
                t2 = src3[:, :, 8:16]
                rr = [rtt[:, a, 0:nh, :] for a in range(4)]
                P.op("dve", lambda e, t1=t1, c_b=c_b, rr=rr: e.tensor_mul(out=rr[0], in0=t1, in1=c_b),
                     reads=bsrc + [self.b_cs], writes=[b_rt])
                P.op("dve", lambda e, t2=t2, s_b=s_b, rr=rr: e.tensor_mul(out=rr[1], in0=t2, in1=s_b),
                     reads=bsrc + [self.b_cs], writes=[b_rt])
                P.op("dve", lambda e, t2=t2, c_b=c_b, rr=rr: e.tensor_mul(out=rr[2], in0=t2, in1=c_b),
                     reads=bsrc + [self.b_cs], writes=[b_rt])
                P.op("dve", lambda e, t1=t1, s_b=s_b, rr=rr: e.tensor_mul(out=rr[3], in0=t1, in1=s_b),
                     reads=bsrc + [self.b_cs], writes=[b_rt])
                P.op("dve", lambda e, dst3=dst3, rr=rr: e.tensor_sub(out=dst3[:, :, 0:8], in0=rr[0], in1=rr[1]),
                     reads=[b_rt], writes=[b_qk[s2]])
                P.op("dve", lambda e, dst3=dst3, rr=rr: e.tensor_add(out=dst3[:, :, 8:16], in0=rr[2], in1=rr[3]),
                     reads=[b_rt], writes=[b_qk[s2]])
            if kind == 2:
                src3 = self.bank(2)[:, 0:128].rearrange("p (g c) -> p g c", g=2)
                kd4 = qk[:, 1024:1280].rearrange("p (g r c) -> p g r c", g=2, r=2)
                c_b = cs_t[:, 0:8].unsqueeze(1).to_broadcast([128, 2, 8])
                s_b = cs_t[:, 8:16].unsqueeze(1).to_broadcast([128, 2, 8])
                t1 = src3[:, :, 0:8]
                t2 = src3[:, :, 8:16]
                rr = [rtt[:, a, 0:2, :] for a in range(4)]
                bsrc = [self.b_ps[2]]
                P.op("dve", lambda e, t1=t1, c_b=c_b, rr=rr: e.tensor_mul(out=rr[0], in0=t1, in1=c_b),
                     reads=bsrc + [self.b_cs], writes=[b_rt])
                P.op("dve", lambda e, t2=t2, s_b=s_b, rr=rr: e.tensor_mul(out=rr[1], in0=t2, in1=s_b),
                     reads=bsrc + [self.b_cs], writes=[b_rt])
                P.op("dve", lambda e, t2=t2, c_b=c_b, rr=rr: e.tensor_mul(out=rr[2], in0=t2, in1=c_b),
                     reads=bsrc + [self.b_cs], writes=[b_rt])
                P.op("dve", lambda e, t1=t1, s_b=s_b, rr=rr: e.tensor_mul(out=rr[3], in0=t1, in1=s_b),
                     reads=bsrc + [self.b_cs], writes=[b_rt])
                for r in range(2):
                    P.op("dve", lambda e, r=r, kd4=kd4, rr=rr: e.tensor_sub(
                        out=kd4[:, :, r, 0:8], in0=rr[0], in1=rr[1]), reads=[b_rt], writes=[b_qk[s2]])
                    P.op("dve", lambda e, r=r, kd4=kd4, rr=rr: e.tensor_add(
                        out=kd4[:, :, r, 8:16], in0=rr[2], in1=rr[3]), reads=[b_rt], writes=[b_qk[s2]])
            if _DBG_M1STOP <= 4:
                continue
            vs = v_sb[s2]
            if kind != 2:
                for hb in range(2):
                    P.op("dve", lambda e, vs=vs, hb=hb: e.tensor_copy(out=vs[:, hb * 512:(hb + 1) * 512],
                                                                     in_=self.bank(4 + hb)),
                         reads=self.b_ps[4:6], writes=[b_v[s2]])
                P.dma("sp", lambda e, vs=vs, t=t: e.dma_start(out=self.v_d[t * 128:(t + 1) * 128, :], in_=vs),
                      b_v[s2], writes=[self.b_vd[t]])
            else:
                P.op("dve", lambda e, vs=vs: e.tensor_copy(out=vs[:, 0:128], in_=self.bank(2)[:, 128:256]),
                     reads=[self.b_ps[2]], writes=[b_v[s2]])
                P.dma("sp", lambda e, vs=vs, t=t: e.dma_start(out=self.v_d[t * 128:(t + 1) * 128, 0:128],
                                                             in_=vs[:, 0:128]),
                      b_v[s2], writes=[self.b_vd[t]])
            if _DBG_M1STOP <= 5:
                continue
            g4 = t // 4
            x4 = t % 4
            sg = g4 % 2
            if own:
                pb = self.bank_bf(6 + s2).rearrange("p (k t) -> p k t", k=8)
                for c in range(8):
                    P.op("pe", lambda e, c=c, pb=pb, qk=qk: e.transpose(
                        out=pb[:, c, :], in_=qk[:, c * 128:(c + 1) * 128], identity=self.ident[:]),
                        reads=[b_qk[s2], self.b_const], writes=[self.b_ps[6 + s2]])
                P.op("act", lambda e, pb=pb, sg=sg, x4=x4: e.copy(
                    out=qT_st[sg][:, :, x4 * 128:(x4 + 1) * 128], in_=pb),
                    reads=[self.b_ps[6 + s2]], writes=[b_qT[sg]])
            pb = self.bank_bf(6 + s2).rearrange("p (k t) -> p k t", k=8)
            koff = 1024
            for c in range(nkc if kind != 2 else 2):
                P.op("pe", lambda e, c=c, pb=pb, qk=qk: e.transpose(
                    out=pb[:, c, :], in_=qk[:, koff + c * 128:koff + (c + 1) * 128], identity=self.ident[:]),
                    reads=[b_qk[s2], self.b_const], writes=[self.b_ps[6 + s2]])
            nkt = nkc if kind != 2 else 2
            P.op("dve", lambda e, pb=pb, sg=sg, x4=x4, nkt=nkt: e.tensor_copy(
                out=kT_st[sg][:, 0:nkt, x4 * 128:(x4 + 1) * 128], in_=pb[:, 0:nkt, :]),
                reads=[self.b_ps[6 + s2]], writes=[b_kT[sg]])
            if _DBG_M1STOP <= 6:
                continue
            last_in_group = (x4 == 3) or (it == len(tiles) - 1)
            if last_in_group:
                ntk = (x4 + 1) * 128
                if own:
                    P.dma("sp", lambda e, sg=sg, g4=g4: e.dma_start(
                        out=self.qT_d[:, :, g4 * 512:(g4 + 1) * 512].rearrange("c p t -> p c t"),
                        in_=qT_st[sg]), b_qT[sg], writes=[self.b_qTd[g4]])
                P.dma("sp", lambda e, sg=sg, g4=g4, ntk=ntk, nkt=nkt: e.dma_start(
                    out=self.kT_d[0:nkt, :, g4 * 512:g4 * 512 + ntk].rearrange("c p t -> p c t"),
                    in_=kT_st[sg][:, 0:nkt, 0:ntk]), b_kT[sg], writes=[self.b_kTd[g4]])
        P.barrier()

    def stage_m2_banded(self, l):
        P = self.P
        kind = layer_kind(l)
        w = self.W[l]
        self.areset()
        OT = self.abf(8 * HALF).rearrange("p (c t) -> p c t", c=8)
        self.OT = OT
        self.b_OT = P.buf("OT")
        if kind == 0:
            dils = A_DILS
            R = 64
            PADK = 64 * 16
            NKTOK = HALF + 1024
        else:
            dils = (1,)
            R = 128
            PADK = 128
            NKTOK = HALF + 128
        nkt_per_q = 2 if kind == 0 else 3
        ntl = {d: (HALF // d + 2 * R) // 128 for d in dils}
        voff = {}
        tot = 0
        for d in dils:
            voff[d] = tot
            tot += d * ntl[d]
        NB = 2
        QTs = [self.abf(HALF)] * NB
        KTs = [self.abf(PADK + NKTOK) for _ in range(NB)]
        Vps = [self.abf(tot * 128).rearrange("p (j c) -> p j c", c=128) for _ in range(NB)]
        b_Q = [P.buf("Q")] * NB
        pcount = 0
        b_K = P.bufs(NB, "K")
        b_V = P.bufs(NB, "V")
        accO = self.af32(HALF)
        accD = self.af32(HALF)
        b_acc = P.buf("acc")
        NPT = 4 if kind == 0 else 6
        Pt = [self.abf(512) for _ in range(NPT)]
        b_Pt = P.bufs(NPT, "Pt")
        rec = self.af32(512)
        b_rec = P.buf("rec")
        tO = self.af32(512)
        tD = self.af32(512)
        b_tO = P.buf("tO")
        es_t = self.small[:, 32:40]
        b_es = P.buf("es")
        if kind == 2:
            sk = w["sink"]
            for hp in range(8):
                for h2 in range(2):
                    P.dma("sp", lambda e, hp=hp, h2=h2: e.dma_start(
                        out=es_t[h2 * 64:(h2 + 1) * 64, hp:hp + 1],
                        in_=sk[2 * hp + h2:2 * hp + h2 + 1, :].partition_broadcast(64)), b_es)
            P.op("act", lambda e: e.activation(out=es_t, in_=es_t, func=AF.Exp), reads=[b_es], writes=[b_es])
        for i in range(NB):
            P.op("pool", lambda e, i=i: e.memset(KTs[i][:, 0:PADK], 0.0), writes=[b_K[i]])
            P.op("pool", lambda e, i=i: e.memset(Vps[i], 0.0), writes=[b_V[i]])
        scale = 1.0 / 8.0
        for hp in range(8):
            sl = hp % NB
            QT, KT, Vp = QTs[sl], KTs[sl], Vps[sl]
            if kind == 0:
                P.dma("sp", lambda e, hp=hp, KT=KT: e.dma_start(out=KT[:, PADK:PADK + NKTOK],
                                                               in_=self.kT_d[hp][:, 0:NKTOK]),
                      b_K[sl], reads=self.b_kTd)
            else:
                kv = hp // 4
                P.dma("sp", lambda e, kv=kv, KT=KT: e.dma_start(out=KT[:, PADK:PADK + NKTOK],
                                                               in_=self.kT_d[kv][:, 0:NKTOK]),
                      b_K[sl], reads=self.b_kTd)
            for d in dils:
                for r in range(d):
                    base_t = voff[d] + r * ntl[d]
                    n1 = ntl[d] - 1
                    if kind == 0:
                        cols = slice(hp * 128, (hp + 1) * 128)
                        src = self.v_d[ssl(R * d + r, n1 * 128, d), cols]
                        P.dma("sp", lambda e, src=src, Vp=Vp, base_t=base_t, n1=n1: e.dma_start(
                            out=Vp[:, base_t + 1:base_t + 1 + n1, :],
                            in_=src.rearrange("(j i) c -> i j c", i=128)), b_V[sl], reads=self.b_vd)
                        src0 = self.v_d[ssl(r, 128 - R, d), cols]
                        P.dma("sp", lambda e, src0=src0, Vp=Vp, base_t=base_t: e.dma_start(
                            out=Vp[R:128, base_t, :], in_=src0), b_V[sl], reads=self.b_vd)
                    else:
                        kv = hp // 4
                        for dup in range(2):
                            src = self.v_d[0:n1 * 128, kv * 64:(kv + 1) * 64]
                            P.dma("sp", lambda e, src=src, Vp=Vp, base_t=base_t, n1=n1, dup=dup: e.dma_start(
                                out=Vp[:, base_t + 1:base_t + 1 + n1, dup * 64:(dup + 1) * 64],
                                in_=src.rearrange("(j i) c -> i j c", i=128)), b_V[sl], reads=self.b_vd)
            P.dma("sp", lambda e, hp=hp, QT=QT: e.dma_start(out=QT, in_=self.qT_d[hp]), b_Q[sl],
                  reads=self.b_qTd)
            if _DBG_M2STOP <= 0:
                continue
            groups = [(h2, pi, d, qg) for h2 in range(2) for pi, d in enumerate(dils) for qg in range(8)]
            sched = []
            for gi in range(len(groups)):
                if gi == 0:
                    sched.append((0, 0))
                if gi + 1 < len(groups):
                    sched.append((gi + 1, 0))
                sched.append((gi, 1))
            if _DBG_M2STOP <= 0:
                sched = []
            for gidx, which in sched:
                for (h2, pi, d, qg) in (groups[gidx],):
                    ph = slice(h2 * 64, (h2 + 1) * 64)
                    if True:
                        nq = 32 // d
                        qis = [qg * 4 + x for x in range(4)]
                        rj = [divmod(qi, nq) for qi in qis]
                        bO, bD = 4 + 2 * (gidx % 2), 5 + 2 * (gidx % 2)
                        for phase, kt in [(which, k_) for k_ in range(nkt_per_q)]:
                            sb_i = (gidx * nkt_per_q + kt) % 4
                            skip = [False] * 4
                            for x, (r, j) in enumerate(rj):
                                jp = j + kt
                                if kind == 2 and jp == 0:
                                    skip[x] = True
                            pt_i = (gidx * nkt_per_q + kt) % NPT
                            PT = Pt[pt_i]
                            if phase == 1:
                                if kt != nkt_per_q - 1:
                                    continue
                                for x, (r, j) in enumerate(rj):
                                    for kt2 in range(nkt_per_q):
                                        pt2 = (gidx * nkt_per_q + kt2) % NPT
                                        vt = voff[d] + r * ntl[d] + j + kt2
                                        P.op("pe", lambda e, x=x, vt=vt, Vp=Vp, PT2=Pt[pt2], kt2=kt2, bO=bO: e.matmul(
                                            self.bank(bO)[:, x * 128:(x + 1) * 128], lhsT=Vp[:, vt, :],
                                            rhs=PT2[:, x * 128:(x + 1) * 128],
                                            start=(kt2 == 0), stop=(kt2 == nkt_per_q - 1)),
                                            reads=[b_V[sl], b_Pt[pt2]], writes=[self.b_ps[bO]])
                                for kt2 in range(nkt_per_q):
                                    pt2 = (gidx * nkt_per_q + kt2) % NPT
                                    P.op("pe", lambda e, PT2=Pt[pt2], kt2=kt2, bD=bD: e.matmul(
                                        self.bank(bD), lhsT=self.ones_bf[:], rhs=PT2,
                                        start=(kt2 == 0), stop=(kt2 == nkt_per_q - 1)),
                                        reads=[self.b_const, b_Pt[pt2]], writes=[self.b_ps[bD]])
                                continue
                            for x, (r, j) in enumerate(rj):
                                jp = j + kt
                                kc0 = PADK + (jp * 128 - R) * d + r
                                qc0 = j * 128 * d + r
                                P.op("pe", lambda e, x=x, kc0=kc0, qc0=qc0, d=d, KT=KT, QT=QT, ph=ph, sb_i=sb_i: e.matmul(
                                    self.bank(sb_i)[:, x * 128:(x + 1) * 128],
                                    lhsT=KT[ph, ssl(kc0, 128, d)], rhs=QT[ph, ssl(qc0, 128, d)],
                                    start=True, stop=True),
                                    reads=[b_K[sl], b_Q[sl]], writes=[self.b_ps[sb_i]])
                            P.op("act", lambda e, PT=PT, sb_i=sb_i: e.activation(
                                out=PT, in_=self.bank(sb_i), func=AF.Exp, scale=scale),
                                reads=[self.b_ps[sb_i]], writes=[b_Pt[pt_i]])
                            if _DBG_M2STOP <= 1:
                                continue
                            PT3 = PT.rearrange("p (x q) -> p x q", x=4)
                            if kt == 0:
                                P.op("pool", lambda e, PT3=PT3: e.affine_select(
                                    out=PT3, in_=PT3, pattern=[[0, 4], [-1, 128]], compare_op=ALU.is_ge,
                                    fill=0.0, base=0, channel_multiplier=1),
                                    reads=[b_Pt[pt_i]], writes=[b_Pt[pt_i]])
                            elif kt == nkt_per_q - 1:
                                P.op("pool", lambda e, PT3=PT3: e.affine_select(
                                    out=PT3, in_=PT3, pattern=[[0, 4], [1, 128]], compare_op=ALU.is_ge,
                                    fill=0.0, base=0, channel_multiplier=-1),
                                    reads=[b_Pt[pt_i]], writes=[b_Pt[pt_i]])
                            for x, (r, j) in enumerate(rj):
                                jp = j + kt
                                if jp == 0:
                                    if kind == 0:
                                        P.op("pool", lambda e, PT3=PT3, x=x: e.affine_select(
                                            out=PT3[:, x, :], in_=PT3[:, x, :], pattern=[[0, 128]],
                                            compare_op=ALU.is_ge, fill=0.0, base=-R, channel_multiplier=1),
                                            reads=[b_Pt[pt_i]], writes=[b_Pt[pt_i]])
                                    else:
                                        P.op("pool", lambda e, PT3=PT3, x=x: e.memset(PT3[:, x, :], 0.0),
                                             writes=[b_Pt[pt_i]])
                        if which == 0 or _DBG_M2STOP <= 3:
                            continue
                        if d == 1:
                            dO = accO[ph, qg * 512:(qg + 1) * 512]
                            dD = accD[ph, qg * 512:(qg + 1) * 512]
                            sO = self.bank(bO)[ph, :]
                            sD = self.bank(bD)[ph, :]
                        elif d == 4:
                            r0, j0 = rj[0]
                            o0 = r0 + j0 * 512
                            dO = accO[ph, ssl(o0, 512, 4)].rearrange("p (x i) -> p x i", x=4)
                            dD = accD[ph, ssl(o0, 512, 4)].rearrange("p (x i) -> p x i", x=4)
                            sO = self.bank(bO)[ph, :].rearrange("p (x i) -> p x i", x=4)
                            sD = self.bank(bD)[ph, :].rearrange("p (x i) -> p x i", x=4)
                        else:
                            r0, j0 = rj[0]
                            assert j0 == 0
                            dO = [accO[ph, ssl(r_ + j_ * 2048, 128, 16)] for (r_, j_) in rj]
                            dD = [accD[ph, ssl(r_ + j_ * 2048, 128, 16)] for (r_, j_) in rj]
                            sO = [self.bank(bO)[ph, x * 128:(x + 1) * 128] for x in range(4)]
                            sD = [self.bank(bD)[ph, x * 128:(x + 1) * 128] for x in range(4)]
                        if d == 1:
                            P.op("dve", lambda e, dO=dO, sO=sO: e.tensor_copy(out=dO, in_=sO),
                                 reads=[self.b_ps[bO]], writes=[b_acc])
                            P.op("dve", lambda e, dD=dD, sD=sD: e.tensor_copy(out=dD, in_=sD),
                                 reads=[self.b_ps[bD]], writes=[b_acc])
                        elif d == 4:
                            P.op("dve", lambda e, dO=dO, sO=sO: e.tensor_add(out=dO, in0=dO, in1=sO),
                                 reads=[self.b_ps[bO]], writes=[b_acc])
                            P.op("dve", lambda e, dD=dD, sD=sD: e.tensor_add(out=dD, in0=dD, in1=sD),
                                 reads=[self.b_ps[bD]], writes=[b_acc])
                        else:
                            P.op("act", lambda e, bO=bO, ph=ph: e.copy(out=tO[ph, :], in_=self.bank(bO)[ph, :]),
                                 reads=[self.b_ps[bO]], writes=[b_tO])
                            P.op("act", lambda e, bD=bD, ph=ph: e.copy(out=tD[ph, :], in_=self.bank(bD)[ph, :]),
                                 reads=[self.b_ps[bD]], writes=[b_tO])
                            for rr in range(4):
                                P.op("pool", lambda e, a=dO[rr], rr=rr, ph=ph: e.tensor_add(
                                    out=a, in0=a, in1=tO[ph, rr * 128:(rr + 1) * 128]),
                                    reads=[b_tO], writes=[b_acc])
                                P.op("pool", lambda e, a=dD[rr], rr=rr, ph=ph: e.tensor_add(
                                    out=a, in0=a, in1=tD[ph, rr * 128:(rr + 1) * 128]),
                                    reads=[b_tO], writes=[b_acc])
            if _DBG_M2STOP <= 4:
                continue
            if kind == 2:
                P.op("dve", lambda e, hp=hp: e.tensor_scalar(out=accD, in0=accD, scalar1=es_t[:, hp:hp + 1],
                                                             scalar2=None, op0=ALU.add),
                     reads=[b_es], writes=[b_acc])
            for qb in range(8):
                cs_ = slice(qb * 512, (qb + 1) * 512)
                P.op("dve", lambda e, cs_=cs_: e.reciprocal(out=rec, in_=accD[:, cs_]), reads=[b_acc], writes=[b_rec])
                P.op("pool", lambda e, cs_=cs_, hp=hp: e.tensor_mul(out=OT[:, hp, cs_], in0=accO[:, cs_], in1=rec),
                     reads=[b_acc, b_rec], writes=[self.b_OT])
        P.barrier()

    def stage_m2_dense(self, l):
        P = self.P
        w = self.W[l]
        self.areset()
        OT = self.abf(8 * HALF).rearrange("p (c t) -> p c t", c=8)
        self.OT = OT
        self.b_OT = P.buf("OT")
        NB = 2
        QTs = [self.abf(HALF) for _ in range(NB)]
        KTs = [self.abf(S) for _ in range(NB)]
        Vs = [self.abf(64 * 128).rearrange("p (j c) -> p j c", c=128) for _ in range(NB)]
        b_Q = P.bufs(NB, "Q")
        b_K = P.bufs(NB, "K")
        b_V = P.bufs(NB, "V")
        Pt = [self.abf(512) for _ in range(4)]
        b_Pt = P.bufs(4, "Pt")
        r1 = self.af32(512)
        r2 = self.af32(512)
        o1 = self.af32(512)
        o2 = self.af32(512)
        sq = self.af32(512)
        b_fin = P.buf("fin")
        lam_init = 0.8 - 0.6 * math.exp(-0.3 * l)
        lv = self.af32(256)
        b_lv = P.buf("lv")
        P.dma("sp", lambda e: e.dma_start(out=lv, in_=w["lam"].partition_broadcast(128)), b_lv)
        sm = self.small
        b_sm = P.buf("sm")
        pr = self.af32(128)
        P.op("dve", lambda e: e.tensor_mul(out=pr[:, 0:64], in0=lv[:, 0:64], in1=lv[:, 64:128]),
             reads=[b_lv], writes=[b_sm])
        P.op("dve", lambda e: e.tensor_mul(out=pr[:, 64:128], in0=lv[:, 128:192], in1=lv[:, 192:256]),
             reads=[b_lv], writes=[b_sm])
        P.op("dve", lambda e: e.reduce_sum(out=sm[:, 40:42], in_=pr.rearrange("p (a c) -> p a c", a=2), axis=AX.X),
             reads=[b_sm], writes=[b_sm])
        P.op("act", lambda e: e.activation(out=sm[:, 42:44], in_=sm[:, 40:42], func=AF.Exp), reads=[b_sm], writes=[b_sm])
        P.op("dve", lambda e: e.tensor_sub(out=sm[:, 44:45], in0=sm[:, 43:44], in1=sm[:, 42:43]),
             reads=[b_sm], writes=[b_sm])
        P.op("dve", lambda e: e.tensor_scalar(out=sm[:, 44:45], in0=sm[:, 44:45], scalar1=-lam_init, scalar2=None,
                                               op0=ALU.add), reads=[b_sm], writes=[b_sm])
        neglam = sm[:, 44:45]
        P.dma("sp", lambda e: e.dma_start(out=sm[:, 45:46], in_=w["subg"]), b_sm)
        P.op("dve", lambda e: e.tensor_scalar(out=sm[:, 45:46], in0=sm[:, 45:46], scalar1=1.0 - lam_init,
                                               scalar2=None, op0=ALU.mult), reads=[b_sm], writes=[b_sm])
        subg = sm[:, 45:46]
        scale = 1.0 / 8.0
        for h in range(8):
            sl = h % NB
            QT, KT, V = QTs[sl], KTs[sl], Vs[sl]
            P.dma("sp", lambda e, h=h, QT=QT: e.dma_start(out=QT, in_=self.qT_d[h]), b_Q[sl], reads=self.b_qTd)
            P.dma("sp", lambda e, h=h, KT=KT: e.dma_start(out=KT, in_=self.kT_d[h]), b_K[sl], reads=self.b_kTd)
            for half in range(2):
                src = self.v_d[half * 4096:(half + 1) * 4096, h * 128:(h + 1) * 128]
                P.dma("sp", lambda e, src=src, V=V, half=half: e.dma_start(
                    out=V[:, half * 32:(half + 1) * 32, :], in_=src.rearrange("(j i) c -> i j c", i=128)),
                    b_V[sl], reads=self.b_vd)
            for qb in range(8):
                qs = slice(qb * 512, (qb + 1) * 512)

                def qk_exp(kt, qs=qs, KT=KT, QT=QT, sl=sl):
                    ks = slice(kt * 128, (kt + 1) * 128)
                    pp = kt % 2
                    for c in range(2):
                        bk = pp * 2 + c
                        ph = slice(c * 64, (c + 1) * 64)
                        P.op("pe", lambda e, bk=bk, ph=ph, ks=ks: e.matmul(
                            self.bank(bk), lhsT=KT[ph, ks], rhs=QT[ph, qs], start=True, stop=True),
                            reads=[b_K[sl], b_Q[sl]], writes=[self.b_ps[bk]])
                    for c in range(2):
                        bk = pp * 2 + c
                        P.op("act", lambda e, bk=bk: e.activation(out=Pt[bk], in_=self.bank(bk), func=AF.Exp,
                                                                   scale=scale),
                             reads=[self.b_ps[bk]], writes=[b_Pt[bk]])

                def pv(kt, V=V, sl=sl):
                    pp = kt % 2
                    for c in range(2):
                        bk = pp * 2 + c
                        P.op("pe", lambda e, bk=bk, c=c, kt=kt: e.matmul(
                            self.bank(4 + c), lhsT=V[:, kt, :], rhs=Pt[bk], start=(kt == 0), stop=(kt == 63)),
                            reads=[b_V[sl], b_Pt[bk]], writes=[self.b_ps[4 + c]])
                        P.op("pe", lambda e, bk=bk, c=c, kt=kt: e.matmul(
                            self.bank(6 + c), lhsT=self.ones_bf[:], rhs=Pt[bk], start=(kt == 0), stop=(kt == 63)),
                            reads=[self.b_const, b_Pt[bk]], writes=[self.b_ps[6 + c]])

                qk_exp(0)
                for kt in range(64):
                    if kt + 1 < 64:
                        qk_exp(kt + 1)
                    pv(kt)
                P.op("dve", lambda e: e.reciprocal(out=r1, in_=self.bank(6)), reads=[self.b_ps[6]], writes=[b_fin])
                P.op("dve", lambda e: e.reciprocal(out=r2, in_=self.bank(7)), reads=[self.b_ps[7]], writes=[b_fin])
                P.op("dve", lambda e: e.tensor_mul(out=o1, in0=self.bank(4), in1=r1), reads=[self.b_ps[4], b_fin],
                     writes=[b_fin])
                P.op("dve", lambda e: e.tensor_mul(out=o2, in0=self.bank(5), in1=r2), reads=[self.b_ps[5], b_fin],
                     writes=[b_fin])
                P.op("dve", lambda e: e.scalar_tensor_tensor(out=o1, in0=o2, scalar=neglam, in1=o1,
                                                              op0=ALU.mult, op1=ALU.add),
                     reads=[b_fin, b_sm], writes=[b_fin])
                P.op("act", lambda e: e.activation(out=sq, in_=o1, func=AF.Square), reads=[b_fin], writes=[b_fin])
                P.op("pe", lambda e: e.matmul(self.bank(0), lhsT=self.ones_f[:], rhs=sq, start=True, stop=True),
                     reads=[self.b_const, b_fin], writes=[self.b_ps[0]])
                P.op("act", lambda e: e.activation(out=r1, in_=self.bank(0), func=AF.Sqrt, scale=1.0 / 128, bias=EPS),
                     reads=[self.b_ps[0]], writes=[b_fin])
                P.op("dve", lambda e: e.reciprocal(out=r1, in_=r1), reads=[b_fin], writes=[b_fin])
                P.op("dve", lambda e, h=h, qs=qs: e.scalar_tensor_tensor(
                    out=OT[:, h, qs], in0=o1, scalar=subg, in1=r1, op0=ALU.mult, op1=ALU.mult),
                    reads=[b_fin, b_sm], writes=[self.b_OT])
        P.barrier()

    def stage_s1(self, l, hsrc, b_hsrc):
        P = self.P
        w = self.W[l]
        OT, b_OT = self.OT, self.b_OT
        self.apos = 8 * HALF
        if _DBG_H1:
            for c in range(8):
                P.dma("sp", lambda e, c=c: e.dma_start(out=self.ot_d[:, c, :], in_=OT[:, c, :]), b_OT)
        w_out = self.abf(8 * D).rearrange("p (k n) -> p k n", k=8)
        wq = self.abf(8 * D).rearrange("p (k n) -> p k n", k=8)
        wo = self.abf(8 * D).rearrange("p (k n) -> p k n", k=8)
        b_wout, b_wq, b_wo = P.buf("wout"), P.buf("wq"), P.buf("wo")
        self.load_w(w_out, w["w_out"], b_wout, 8)
        self.load_w(wq, w["wq"], b_wq, 8)
        self.load_w(wo, w["wo"], b_wo, 8)
        g_post = self.af32(D)
        g_mpre = self.af32(D)
        g_mpost = self.af32(D)
        g_kv = self.af32(D)
        b_g = P.bufs(4, "g")
        self.load_g(g_post, w["g"], 1, b_g[0])
        self.load_g(g_mpre, w["g"], 2, b_g[1])
        self.load_g(g_kv, w["g"], 3, b_g[2])
        self.load_g(g_mpost, w["g"], 4, b_g[3])
        KTm = self.abf(8 * NMEM).rearrange("p (c t) -> p c t", c=8)
        Vm = self.abf(2 * D).rearrange("p (j c) -> p j c", j=2)
        b_KTm, b_Vm = P.buf("KTm"), P.buf("Vm")
        hres = [self.af32(D) for _ in range(4)]
        b_h = P.bufs(4, "h")
        tmp = self.af32(D)
        b_tmp = P.buf("tmp")
        u2 = [self.abf(D) for _ in range(2)]
        b_u2 = P.bufs(2, "u2")
        sm = self.small
        b_ss = P.bufs(4, "ss")
        mark = self.apos
        wkv = self.abf(8 * 2 * D).rearrange("p (k n) -> p k n", k=8)
        b_wkv = P.buf("wkv")
        self.load_w(wkv, w["wkv"], b_wkv, 8)
        uTm = self.abf(8 * NMEM).rearrange("p (k t) -> p k t", k=8)
        b_uTm = P.bufs(2, "uTm")
        for mt in range(2):
            ht = hres[mt]
            P.dma("sp", lambda e, mt=mt, ht=ht: e.dma_start(out=ht, in_=self.mem[mt * 128:(mt + 1) * 128, :]), b_h[mt])
            self.rms_u(ht, b_h[mt], g_kv, b_g[2], u2[mt], b_u2[mt], sm[:, 2 * mt:2 * mt + 1],
                       sm[:, 2 * mt + 1:2 * mt + 2], b_ss[mt], tmp.bitcast(BF16)[:, 0:D], b_tmp)
            self.transpose8(u2[mt], b_u2[mt], 2, uTm[:, :, mt * 128:(mt + 1) * 128], b_uTm[mt])
        for c in range(8):
            bk = 3 + c % 2
            for k in range(8):
                P.op("pe", lambda e, c=c, k=k, bk=bk: e.matmul(
                    self.bank(bk)[:, 0:NMEM], lhsT=wkv[:, k, c * 128:(c + 1) * 128], rhs=uTm[:, k, :],
                    start=(k == 0), stop=(k == 7)), reads=[b_wkv] + b_uTm, writes=[self.b_ps[bk]])
            P.op("act", lambda e, c=c, bk=bk: e.copy(out=KTm[:, c, :], in_=self.bank(bk)[:, 0:NMEM]),
                 reads=[self.b_ps[bk]], writes=[b_KTm])
        for mt in range(2):
            for cb in range(2):
                for k in range(8):
                    P.op("pe", lambda e, mt=mt, cb=cb, k=k: e.matmul(
                        self.bank(cb), lhsT=uTm[:, k, mt * 128:(mt + 1) * 128],
                        rhs=wkv[:, k, D + cb * 512:D + (cb + 1) * 512], start=(k == 0), stop=(k == 7)),
                        reads=[b_wkv] + b_uTm, writes=[self.b_ps[cb]])
            P.op("act", lambda e, mt=mt: e.copy(out=Vm[:, mt, :], in_=self.bank(0, 2)),
                 reads=self.b_ps[0:2], writes=[b_Vm])
        P.barrier()
        self.apos = mark
        u2T = self.abf(8 * 512).rearrange("p (k t) -> p k t", k=8)
        b_u2T = P.bufs(4, "u2T")
        qTx = self.abf(8 * 512).rearrange("p (k t) -> p k t", k=8)
        b_qTx = P.bufs(8, "qTx")
        oxT = self.abf(8 * 512).rearrange("p (k t) -> p k t", k=8)
        b_oxT = P.bufs(8, "oxT")
        Px = [self.abf(512) for _ in range(2)]
        b_Px = P.bufs(2, "Px")
        rec = self.af32(512)
        b_rec = P.buf("rec")
        xscale = 1.0 / 16.0
        self.b_h1d = P.fresh(32)
        for blk in range(8):
            pend_t = []
            for tt in range(4):
                t = blk * 4 + tt
                ts_ = slice(t * 128, (t + 1) * 128)
                ht = hres[tt]
                P.dma("sp", lambda e, ts_=ts_, ht=ht: e.dma_start(out=ht, in_=hsrc[ts_, :]), b_h[tt],
                      reads=[b_hsrc[min(t, len(b_hsrc) - 1)]])
                yb = 0 if tt % 2 == 0 else 3
                for cb in range(2):
                    for k in range(8):
                        P.op("pe", lambda e, cb=cb, k=k, ts_=ts_, yb=yb: e.matmul(
                            self.bank(yb + cb), lhsT=OT[:, k, ts_], rhs=w_out[:, k, cb * 512:(cb + 1) * 512],
                            start=(k == 0), stop=(k == 7)), reads=[b_OT, b_wout], writes=[self.b_ps[yb + cb]])
                ss = sm[:, 2 * tt:2 * tt + 1]
                rstd = sm[:, 2 * tt + 1:2 * tt + 2]
                self.post_norm_residual(yb, ht, b_h[tt], g_post, b_g[0], ss, rstd, b_ss[tt], tmp, b_tmp)
                s2 = tt % 2
                self.rms_u(ht, b_h[tt], g_mpre, b_g[1], u2[s2], b_u2[s2], ss, rstd, b_ss[tt],
                           tmp.bitcast(BF16)[:, 0:D], b_tmp)
                pend_t.append(lambda s2=s2, tt=tt: self.transpose8(
                    u2[s2], b_u2[s2], 2, u2T[:, :, tt * 128:(tt + 1) * 128], b_u2T[tt], evac="dve"))
                if len(pend_t) == 2:
                    pend_t.pop(0)()
            while pend_t:
                pend_t.pop(0)()
            for c in range(8):
                bk = 3 + c % 2
                for k in range(8):
                    P.op("pe", lambda e, c=c, k=k, bk=bk: e.matmul(
                        self.bank(bk), lhsT=wq[:, k, c * 128:(c + 1) * 128], rhs=u2T[:, k, :],
                        start=(k == 0), stop=(k == 7)), reads=[b_wq] + b_u2T, writes=[self.b_ps[bk]])
                P.op("act", lambda e, c=c, bk=bk: e.copy(out=qTx[:, c, :], in_=self.bank(bk)),
                     reads=[self.b_ps[bk]], writes=[b_qTx[c]])
            for hx in range(4):
                for mt in range(2):
                    bk = 3 + mt
                    for dc in range(2):
                        c = hx * 2 + dc
                        P.op("pe", lambda e, c=c, mt=mt, dc=dc, bk=bk: e.matmul(
                            self.bank(bk), lhsT=KTm[:, c, mt * 128:(mt + 1) * 128], rhs=qTx[:, c, :],
                            start=(dc == 0), stop=(dc == 1)), reads=[b_KTm, b_qTx[c]], writes=[self.b_ps[bk]])
                    P.op("act", lambda e, mt=mt, bk=bk: e.activation(out=Px[mt], in_=self.bank(bk), func=AF.Exp,
                                                                      scale=xscale),
                         reads=[self.b_ps[bk]], writes=[b_Px[mt]])
                for mt in range(2):
                    for dc in range(2):
                        c = hx * 2 + dc
                        P.op("pe", lambda e, c=c, mt=mt, dc=dc: e.matmul(
                            self.bank(5 + dc), lhsT=Vm[:, mt, c * 128:(c + 1) * 128], rhs=Px[mt],
                            start=(mt == 0), stop=(mt == 1)), reads=[b_Vm, b_Px[mt]], writes=[self.b_ps[5 + dc]])
                    P.op("pe", lambda e, mt=mt: e.matmul(
                        self.bank(7), lhsT=self.ones_bf[:], rhs=Px[mt], start=(mt == 0), stop=(mt == 1)),
                        reads=[self.b_const, b_Px[mt]], writes=[self.b_ps[7]])
                P.op("dve", lambda e: e.reciprocal(out=rec, in_=self.bank(7)), reads=[self.b_ps[7]], writes=[b_rec])
                for dc in range(2):
                    c = hx * 2 + dc
                    P.op("dve", lambda e, c=c, dc=dc: e.tensor_mul(out=oxT[:, c, :], in0=self.bank(5 + dc), in1=rec),
                         reads=[self.b_ps[5 + dc], b_rec], writes=[b_oxT[c]])
            for tt in range(4):
                t = blk * 4 + tt
                ht = hres[tt]
                yb = 0 if tt % 2 == 0 else 3
                for cb in range(2):
                    for c in range(8):
                        P.op("pe", lambda e, cb=cb, c=c, tt=tt, yb=yb: e.matmul(
                            self.bank(yb + cb), lhsT=oxT[:, c, tt * 128:(tt + 1) * 128],
                            rhs=wo[:, c, cb * 512:(cb + 1) * 512], start=(c == 0), stop=(c == 7)),
                            reads=[b_oxT[c], b_wo], writes=[self.b_ps[yb + cb]])
                ss = sm[:, 2 * tt:2 * tt + 1]
                rstd = sm[:, 2 * tt + 1:2 * tt + 2]
                self.post_norm_residual(yb, ht, b_h[tt], g_mpost, b_g[3], ss, rstd, b_ss[tt], tmp, b_tmp)
                P.dma("sp", lambda e, t=t, ht=ht: e.dma_start(out=self.h1_d[t * 128:(t + 1) * 128, :], in_=ht),
                      b_h[tt], writes=[self.b_h1d[t]])
        P.barrier()

    def stage_s2(self, l, hdst, b_hdst):
        P = self.P
        w = self.W[l]
        self.areset()
        NF = DFF // 128
        wgu = self.abf(8 * 2 * DFF).rearrange("p (k n) -> p k n", k=8)
        wd = self.abf(NF * D).rearrange("p (k n) -> p k n", k=NF)
        b_wgu, b_wd = P.buf("wgu"), P.buf("wd")
        self.load_w(wgu, w["wgu"], b_wgu, 8)
        self.load_w(wd, w["wd"], b_wd, NF)
        g_pre = self.af32(D)
        g_post = self.af32(D)
        b_g = P.bufs(2, "g")
        self.load_g(g_pre, w["g"], 5, b_g[0])
        self.load_g(g_post, w["g"], 6, b_g[1])
        hres = [self.af32(D) for _ in range(4)]
        b_h = P.bufs(4, "h")
        tmp = self.af32(D)
        b_tmp = P.buf("tmp")
        u3 = [self.abf(D) for _ in range(2)]
        b_u3 = P.bufs(2, "u3")
        u3T = self.abf(8 * 512).rearrange("p (k t) -> p k t", k=8)
        b_u3T = P.bufs(4, "u3T")
        hT = self.abf(NF * 512).rearrange("p (k t) -> p k t", k=NF)
        b_hT = P.bufs(NF, "hT")
        sg = [self.af32(512) for _ in range(2)]
        b_sg = P.bufs(2, "sg")
        sm = self.small
        b_ss = P.bufs(4, "ss")
        for blk in range(8):
            for tt in range(4):
                t = blk * 4 + tt
                ht = hres[tt]
                P.dma("sp", lambda e, t=t, ht=ht: e.dma_start(out=ht, in_=self.h1_d[t * 128:(t + 1) * 128, :]),
                      b_h[tt], reads=[self.b_h1d[t]])
                ss = sm[:, 2 * tt:2 * tt + 1]
                rstd = sm[:, 2 * tt + 1:2 * tt + 2]
                s2 = tt % 2
                self.rms_u(ht, b_h[tt], g_pre, b_g[0], u3[s2], b_u3[s2], ss, rstd, b_ss[tt],
                           tmp.bitcast(BF16)[:, 0:D], b_tmp)
                self.transpose8(u3[s2], b_u3[s2], 2, u3T[:, :, tt * 128:(tt + 1) * 128], b_u3T[tt], evac="dve")
            for _w in range(_WARM_S2):
                P.op("pe", lambda e: e.matmul(self.bank(7), lhsT=self.ones_bf[:], rhs=wgu[:, 0, 0:512],
                                              start=True, stop=True), reads=[b_wgu], writes=[self.b_ps[7]])
            for f in range(NF):
                pp = f % 2
                bg, bu = 3 + pp, 5 + pp
                for k in range(8):
                    P.op("pe", lambda e, f=f, k=k, bg=bg: e.matmul(
                        self.bank(bg), lhsT=wgu[:, k, f * 128:(f + 1) * 128], rhs=u3T[:, k, :],
                        start=(k == 0), stop=(k == 7)), reads=[b_wgu] + b_u3T, writes=[self.b_ps[bg]])
                for k in range(8):
                    P.op("pe", lambda e, f=f, k=k, bu=bu: e.matmul(
                        self.bank(bu), lhsT=wgu[:, k, DFF + f * 128:DFF + (f + 1) * 128], rhs=u3T[:, k, :],
                        start=(k == 0), stop=(k == 7)), reads=[b_wgu] + b_u3T, writes=[self.b_ps[bu]])
                P.op("act", lambda e, pp=pp, bg=bg: e.activation(out=sg[pp], in_=self.bank(bg), func=AF.Silu),
                     reads=[self.b_ps[bg]], writes=[b_sg[pp]])
                P.op("dve", lambda e, pp=pp, bu=bu, f=f: e.tensor_mul(out=hT[:, f, :], in0=self.bank(bu), in1=sg[pp]),
                     reads=[self.b_ps[bu], b_sg[pp]], writes=[b_hT[f]])
            for tt in range(4):
                t = blk * 4 + tt
                ht = hres[tt]
                yb = 0 if tt % 2 == 0 else 3
                for cb in range(2):
                    for f in range(NF):
                        P.op("pe", lambda e, cb=cb, f=f, tt=tt, yb=yb: e.matmul(
                            self.bank(yb + cb), lhsT=hT[:, f, tt * 128:(tt + 1) * 128],
                            rhs=wd[:, f, cb * 512:(cb + 1) * 512], start=(f == 0), stop=(f == NF - 1)),
                            reads=[b_hT[f], b_wd], writes=[self.b_ps[yb + cb]])
                ss = sm[:, 2 * tt:2 * tt + 1]
                rstd = sm[:, 2 * tt + 1:2 * tt + 2]
                self.post_norm_residual(yb, ht, b_h[tt], g_post, b_g[1], ss, rstd, b_ss[tt], tmp, b_tmp)
                P.dma("sp", lambda e, t=t, ht=ht: e.dma_start(out=hdst[t * 128:(t + 1) * 128, :], in_=ht),
                      b_h[tt], writes=[b_hdst[t]])
                self.final_bufs = b_h
        P.barrier()

    def build(self):
        P = self.P
        self.alloc()
        self.consts()
        b_hin = [P.buf("hin")]
        self.final_bufs = []
        st = _DBG_STAGES
        hcur, b_hcur = self.hin, b_hin
        gath, b_gath = None, None
        for li, l in enumerate(self.layers):
            last = li == len(self.layers) - 1
            hdst = self.hout if last else self.hx_d[li % 2]
            b_hdst = P.fresh(32, "hdst")
            if "m1" in st:
                self.stage_m1(l, hcur, b_hcur, gath, b_gath)
            if "m2" in st:
                if layer_kind(l) == 1:
                    self.stage_m2_dense(l)
                else:
                    self.stage_m2_banded(l)
            if "s1" in st:
                self.stage_s1(l, hcur, b_hcur)
            if "s2" in st:
                self.stage_s2(l, hdst, b_hdst)
            if not last:
                ln = self.layers[li + 1]
                n = {0: 8, 1: 32, 2: 4}[layer_kind(ln)] * 128
                gath = self.G_d[ln]
                b_gath = Buf("G")
                cr = min(n, 512)
                for ch in range(n // cr):
                    r0 = HALF - n + ch * cr
                    P.dma("pool", lambda e, hdst=hdst, r0=r0, cr=cr, ch=ch, gath=gath: e.collective_compute(
                        "AllGather", ALU.bypass, replica_groups=[[0, 1], [2, 3], [4, 5], [6, 7]],
                        ins=[hdst[r0:r0 + cr, :].opt()], outs=[gath[ch * 2 * cr:(ch + 1) * 2 * cr, :].opt()]),
                        P.buf("cc"), reads=b_hdst, writes=[b_gath], inc=1)
                P.barrier()
                hcur, b_hcur = hdst, b_hdst
        n = P.finish(self.final_bufs)
        self.es.close()
        return n


_PROG_CACHE = {}
_WARM_S2 = 0
_DBG_M1T = None
_DBG_H1 = False
_DBG_M2STOP = 99
_DBG_M1STOP = 99
_DBG_SKIP = ()
_DBG_STAGES = ("m1", "m2", "s1", "s2")


def _get_prog(layers):
    key = tuple(layers)
    if key not in _PROG_CACHE:
        b = Builder(list(layers), 0)
        b.build()
        _PROG_CACHE[key] = (b, list(layers))
    return _PROG_CACHE[key]


def _layer_inputs(l, lname, p):
    k, j = l % 3, l // 3
    g = np.stack([p["mix_pre_g"][l], p["mix_post_g"][l], p["mem_pre_g"][l], p["mem_kv_g"][l],
                  p["mem_post_g"][l], p["ffn_pre_g"][l], p["ffn_post_g"][l]]).astype(np.float32)
    d = {f"g{lname}": g}
    if k == 0:
        d[f"w_in{lname}"] = p["a_w_in"][j]
        d[f"w_out{lname}"] = p["a_w_out"][j]
    elif k == 1:
        d[f"w_in{lname}"] = p["b_w_in"][j]
        d[f"w_out{lname}"] = p["b_w_out"][j]
        d[f"lam{lname}"] = np.concatenate([p["b_lam_q1"][j], p["b_lam_k1"][j], p["b_lam_q2"][j],
                                           p["b_lam_k2"][j]]).reshape(1, 256).astype(np.float32)
        d[f"subg{lname}"] = p["b_sub_g"][j].reshape(128, 1).astype(np.float32)
    else:
        d[f"w_in{lname}"] = p["c_w_in"][j]
        d[f"w_out{lname}"] = p["c_w_out"][j]
        d[f"sink{lname}"] = p["c_sink"][j].reshape(16, 1).astype(np.float32)
    d[f"wq{lname}"] = p["x_wq"][l]
    d[f"wkv{lname}"] = p["x_wkv"][l]
    d[f"wo{lname}"] = p["x_wo"][l]
    d[f"wgu{lname}"] = p["w_gate_up"][l]
    d[f"wd{lname}"] = p["w_down"][l]
    return {k_: np.ascontiguousarray(v, dtype=np.float32) for k_, v in d.items()}


def _local_order(a, half):
    if half == 0:
        return a
    return a[::-1]


def run_layers(layers, h, mem, positions, params):
    b, _ = _get_prog(list(layers))
    wl = {}
    for l in layers:
        wl.update(_layer_inputs(l, l, params))
    in_maps = []
    for c in range(8):
        bi, half = c // 2, c % 2
        hl = np.ascontiguousarray(_local_order(h[bi], half))
        pl = np.ascontiguousarray(_local_order(positions[bi], half)).astype(np.int32)
        sel = np.zeros((128, 2), np.float32)
        sel[:, 1 - half] = 1.0
        m = {"hin": hl, "pos": np.ascontiguousarray(pl.reshape(64, 128).T), "mem": np.ascontiguousarray(mem[bi]),
             "sel": sel}
        m.update(wl)
        in_maps.append(m)
    res = run_bass_kernel_spmd(b.nc, in_maps, core_ids=list(range(8)))
    out = np.empty_like(h)
    for c in range(8):
        bi, half = c // 2, c % 2
        o = res.results[c]["hout"]
        if half == 0:
            out[bi, :HALF] = o
        else:
            out[bi, HALF:] = o[::-1]
    return out


def kernel(**inputs):
    p = {k: np.asarray(v) for k, v in inputs.items()}
    h = np.ascontiguousarray(p["x"], dtype=np.float32)
    mem = np.ascontiguousarray(p["mem"], dtype=np.float32)
    pos = np.asarray(p["positions"]).astype(np.int32)
    return run_layers([0, 1, 2, 3], h, mem, pos, p)
```
